# Optimizing a Trainium2 kernel written in Bass

```python
import jax, jax.numpy as jnp
from jax import lax
import numpy as np

D_MODEL = 1024
BATCH = 32
SEQ = 2048
DEPTH = 1

GRID_W = 64
CTX_LEN = 256
HEAD_DIM = 64
N_Q_HEADS = (D_MODEL // 2) // HEAD_DIM
N_KV_HEADS = max(1, N_Q_HEADS // 4)
GQA_GROUP = N_Q_HEADS // N_KV_HEADS
ATTN_WIDTH = N_Q_HEADS * HEAD_DIM
KV_WIDTH = N_KV_HEADS * HEAD_DIM
CONV_WIDTH = D_MODEL - ATTN_WIDTH
CONV_SIZE = 31
WINDOW = 128
BLOCK = 128
ROPE_BASE = 10000.0
ROT_AXIS_DIM = HEAD_DIM // 2
LN_EPS = 1e-6
NEG_INF = -1e30
ALPHA = (2.0 * DEPTH) ** 0.25
BETA = (8.0 * DEPTH) ** -0.25

Q_OFF = 0
K_OFF = Q_OFF + ATTN_WIDTH
V_OFF = K_OFF + KV_WIDTH
AG_OFF = V_OFF + KV_WIDTH
CA_OFF = AG_OFF + ATTN_WIDTH
CB_OFF = CA_OFF + CONV_WIDTH
CG_OFF = CB_OFF + CONV_WIDTH
IN_WIDTH = CG_OFF + CONV_WIDTH

kernel_name = "hymba_attn_conformer_deepnorm_prefix"


def _norm(x):
    xf = x.astype(jnp.float32)
    mu = jnp.mean(xf, axis=-1, keepdims=True)
    var = jnp.mean(jnp.square(xf - mu), axis=-1, keepdims=True)
    return ((xf - mu) * lax.rsqrt(var + LN_EPS)).astype(x.dtype)


def _layer_norm(x, g, b):
    return _norm(x) * g + b


def _adaln(cond, w_ada, b_ada):
    mod = jax.nn.silu(cond) @ w_ada + b_ada
    return jnp.split(mod, 3, axis=-1)


def _axial_angles(n):
    rows = n // GRID_W
    row = jnp.repeat(jnp.arange(rows, dtype=jnp.float32), GRID_W)
    col = jnp.tile(jnp.arange(GRID_W, dtype=jnp.float32), rows)
    inv_freq = ROPE_BASE ** (-jnp.arange(0, ROT_AXIS_DIM, 2, dtype=jnp.float32) / ROT_AXIS_DIM)
    return row[:, None] * inv_freq, col[:, None] * inv_freq


def _rope_half(x, ang):
    x1, x2 = jnp.split(x, 2, axis=-1)
    cos = jnp.cos(ang)[None, :, None, :].astype(x.dtype)
    sin = jnp.sin(ang)[None, :, None, :].astype(x.dtype)
    return jnp.concatenate([x1 * cos - x2 * sin, x2 * cos + x1 * sin], axis=-1)


def _rope_2d(x, ang_row, ang_col):
    return jnp.concatenate([_rope_half(x[..., :ROT_AXIS_DIM], ang_row),
                            _rope_half(x[..., ROT_AXIS_DIM:], ang_col)], axis=-1)


def _heads(t, n_heads):
    return t.reshape(t.shape[0], t.shape[1], n_heads, HEAD_DIM)


def _softmax_with_sink(parts, sink):
    b, h, g, q = parts[0].shape[:4]
    sink_logit = jnp.broadcast_to(sink.astype(jnp.float32).reshape(1, h, g, 1, 1), (b, h, g, q, 1))
    probs = jax.nn.softmax(jnp.concatenate([sink_logit] + list(parts), axis=-1), axis=-1)
    splits = [int(s) for s in np.cumsum([1] + [p.shape[-1] for p in parts[:-1]])]
    return jnp.split(probs, splits, axis=-1)[1:]


def _windowed_attention(q, k, v, k_ctx, v_ctx, sink):
    b, n = q.shape[:2]
    nb = n // BLOCK
    span = BLOCK + 2 * WINDOW
    qg = (q * (HEAD_DIM ** -0.5)).reshape(b, n, N_KV_HEADS, GQA_GROUP, HEAD_DIM)
    pad = ((0, 0), (WINDOW, WINDOW), (0, 0), (0, 0))
    k_pad = jnp.pad(k, pad)
    v_pad = jnp.pad(v, pad)
    rel = jnp.arange(span)[None, :] - jnp.arange(BLOCK)[:, None]
    band = (rel >= 0) & (rel <= 2 * WINDOW)

    def block(i):
        start = i * BLOCK
        q_blk = lax.dynamic_slice_in_dim(qg, start, BLOCK, axis=1)
        k_blk = lax.dynamic_slice_in_dim(k_pad, start, span, axis=1)
        v_blk = lax.dynamic_slice_in_dim(v_pad, start, span, axis=1)
        key_pos = start - WINDOW + jnp.arange(span)
        valid = band & ((key_pos >= 0) & (key_pos < n))[None, :]
        s_loc = jnp.einsum('bqhgd,bkhd->bhgqk', q_blk, k_blk).astype(jnp.float32)
        s_loc = jnp.where(valid, s_loc, NEG_INF)
        s_ctx = jnp.einsum('bqhgd,bchd->bhgqc', q_blk, k_ctx).astype(jnp.float32)
        p_loc, p_ctx = _softmax_with_sink([s_loc, s_ctx], sink)
        return (jnp.einsum('bhgqk,bkhd->bqhgd', p_loc.astype(v.dtype), v_blk)
                + jnp.einsum('bhgqc,bchd->bqhgd', p_ctx.astype(v.dtype), v_ctx))

    out = lax.map(block, jnp.arange(nb))
    return jnp.moveaxis(out, 0, 1).reshape(b, n, ATTN_WIDTH)


def _context_attention(q, k_ctx, v_ctx, sink):
    b, m = q.shape[:2]
    qg = (q * (HEAD_DIM ** -0.5)).reshape(b, m, N_KV_HEADS, GQA_GROUP, HEAD_DIM)
    s = jnp.einsum('bqhgd,bkhd->bhgqk', qg, k_ctx).astype(jnp.float32)
    (p,) = _softmax_with_sink([s], sink)
    out = jnp.einsum('bhgqk,bkhd->bqhgd', p.astype(v_ctx.dtype), v_ctx)
    return out.reshape(b, m, ATTN_WIDTH)


def _conformer_conv(a, glu_gate, conv_w, conv_b, ln_g, ln_b):
    u = a * jax.nn.sigmoid(glu_gate)
    u = lax.conv_general_dilated(u, conv_w[:, None, :], window_strides=(1,),
                                 padding=[(CONV_SIZE // 2, CONV_SIZE // 2)],
                                 dimension_numbers=('NWC', 'WIO', 'NWC'),
                                 feature_group_count=CONV_WIDTH) + conv_b
    return jax.nn.silu(_layer_norm(u, ln_g, ln_b))


def _branch_output(p, attn, conv_w, conv_b, ln_g, ln_b, w_out):
    conv = _conformer_conv(p[..., CA_OFF:CB_OFF], p[..., CB_OFF:CG_OFF], conv_w, conv_b, ln_g, ln_b)
    mixed = jnp.concatenate([attn * jax.nn.silu(p[..., AG_OFF:CA_OFF]),
                             conv * jax.nn.silu(p[..., CG_OFF:IN_WIDTH])], axis=-1)
    return mixed @ w_out


def setup_inputs(seed: int = 0) -> dict:
    key = jax.random.key(seed)
    ks = jax.random.split(key, 16)
    f32 = jnp.float32
    d = D_MODEL
    return {
        "x": jax.random.normal(ks[0], (BATCH, SEQ, d), f32),
        "c": jax.random.normal(ks[1], (BATCH, d), f32),
        "ctx": jax.random.normal(ks[2], (BATCH, CTX_LEN, d), f32),
        "c_ctx": jax.random.normal(ks[3], (d,), f32),
        "w_ada": 0.5 * jax.random.normal(ks[4], (DEPTH, d, 3 * d), f32) * d ** -0.5,
        "b_ada": 0.01 * jax.random.normal(ks[5], (DEPTH, 3 * d), f32),
        "w_in": jax.random.normal(ks[6], (DEPTH, d, IN_WIDTH), f32) * d ** -0.5,
        "attn_sink": 0.5 * jax.random.normal(ks[7], (DEPTH, N_Q_HEADS), f32),
        "conv_w": jax.random.normal(ks[8], (DEPTH, CONV_SIZE, CONV_WIDTH), f32) * CONV_SIZE ** -0.5,
        "conv_b": 0.01 * jax.random.normal(ks[9], (DEPTH, CONV_WIDTH), f32),
        "conv_ln_g": 1.0 + 0.01 * jax.random.normal(ks[10], (DEPTH, CONV_WIDTH), f32),
        "conv_ln_b": 0.01 * jax.random.normal(ks[11], (DEPTH, CONV_WIDTH), f32),
        "w_out": BETA * jax.random.normal(ks[12], (DEPTH, d, d), f32) * d ** -0.5,
        "post_ln_g": 1.0 + 0.01 * jax.random.normal(ks[13], (DEPTH, d), f32),
        "post_ln_b": 0.01 * jax.random.normal(ks[14], (DEPTH, d), f32),
    }


def reference(x, c, ctx, c_ctx, w_ada, b_ada, w_in, attn_sink, conv_w, conv_b,
              conv_ln_g, conv_ln_b, w_out, post_ln_g, post_ln_b):
    ang_row, ang_col = _axial_angles(x.shape[1])
    for l in range(DEPTH):
        shift, scale, gate = _adaln(c, w_ada[l], b_ada[l])
        shift_c, scale_c, gate_c = _adaln(c_ctx, w_ada[l], b_ada[l])
        h = _norm(x) * (1.0 + scale[:, None, :]) + shift[:, None, :]
        h_ctx = _norm(ctx) * (1.0 + scale_c) + shift_c
        kv_ctx = h_ctx @ w_in[l][:, K_OFF:AG_OFF]
        k_ctx = _heads(kv_ctx[..., :KV_WIDTH], N_KV_HEADS)
        v_ctx = _heads(kv_ctx[..., KV_WIDTH:], N_KV_HEADS)
        p = h @ w_in[l]
        q = _rope_2d(_heads(p[..., Q_OFF:K_OFF], N_Q_HEADS), ang_row, ang_col)
        k = _rope_2d(_heads(p[..., K_OFF:V_OFF], N_KV_HEADS), ang_row, ang_col)
        v = _heads(p[..., V_OFF:AG_OFF], N_KV_HEADS)
        attn = _windowed_attention(q, k, v, k_ctx, v_ctx, attn_sink[l])
        y = _branch_output(p, attn, conv_w[l], conv_b[l], conv_ln_g[l], conv_ln_b[l], w_out[l])
        x_next = _layer_norm(ALPHA * x + gate[:, None, :] * y, post_ln_g[l], post_ln_b[l])
        if l < DEPTH - 1:
            p_c = h_ctx @ w_in[l]
            attn_c = _context_attention(_heads(p_c[..., Q_OFF:K_OFF], N_Q_HEADS), k_ctx, v_ctx, attn_sink[l])
            y_c = _branch_output(p_c, attn_c, conv_w[l], conv_b[l], conv_ln_g[l], conv_ln_b[l], w_out[l])
            ctx = _layer_norm(ALPHA * ctx + gate_c * y_c, post_ln_g[l], post_ln_b[l])
        x = x_next
    return x
```

```python
import contextlib
import numpy as np
import concourse.bass as bass
import concourse.mybir as mybir
from concourse.bass_utils import run_bass_kernel_spmd

F32 = mybir.dt.float32
BF16 = mybir.dt.bfloat16
AF = mybir.ActivationFunctionType
ALU = mybir.AluOpType

D = 1024
INW = 2816
CTXL = 256
NCORES = 8
ALPHA = 2.0 ** 0.25
LN_EPS = 1e-6
QO, KO, VO, AGO, CAO, CBO, CGO = 0, 512, 640, 768, 1280, 1792, 2304

COMPUTE = ("pe", "act", "dve", "pool")
NDMA = 8


class Sched:
    def __init__(self, dma_queues=("sp",)):
        self.ops = []
        self.dma_queues = tuple(dma_queues)

    def add(self, eng, fn, r=(), w=()):
        if not getattr(self, "enabled", True) and fn is not None:
            return
        r, w = list(r), list(w)
        for k in list(r):
            if isinstance(k, tuple) and k[0] == "ps":
                r.remove(k)
                if k not in w:
                    w.append(k)
        self.ops.append({"eng": eng, "fn": fn, "r": tuple(r), "w": tuple(w)})

    def analyse(self):
        ops = self.ops
        last_w, readers = {}, {}
        qcount = {q: 0 for q in self.dma_queues}
        qhist = {q: [] for q in self.dma_queues}
        for i, op in enumerate(ops):
            e = op["eng"]
            isdma = e in self.dma_queues
            strong, weak = set(), set()
            for k in op["r"] + op["w"]:
                if k in last_w:
                    strong.add(last_w[k])
            for k in op["w"]:
                for d in readers.get(k, ()):
                    weak.add(d)
            deps = set()
            for d in strong | weak:
                if d == i:
                    continue
                de = ops[d]["eng"]
                if de == e and not isdma:
                    if e == "pe":
                        continue
                deps.add(d)
            if isdma:
                n = qcount[e]
                op["dma_idx"] = n
                if n >= NDMA:
                    deps.add(qhist[e][n - NDMA])
                qhist[e].append(i)
                qcount[e] = n + 1
            op["deps"] = deps
            for k in op["r"]:
                readers.setdefault(k, []).append(i)
            for k in op["w"]:
                last_w[k] = i
                readers[k] = []
        needed = set()
        for op in ops:
            needed |= op["deps"]
        cnt = {e: 0 for e in COMPUTE}
        for i, op in enumerate(ops):
            e = op["eng"]
            if e in COMPUTE:
                if i in needed:
                    cnt[e] += 1
                    op["tok"] = (e, cnt[e])
                else:
                    op["tok"] = None
            else:
                n = op["dma_idx"]
                op["tok"] = ((e, n % NDMA), 16 * (n // NDMA + 1))

    def emit_engine(self, e, eng, sems):
        ops = self.ops
        waited = {}
        for i, op in enumerate(ops):
            if op["eng"] != e:
                continue
            need = {}
            for d in op["deps"]:
                sk, val = ops[d]["tok"]
                if need.get(sk, 0) < val:
                    need[sk] = val
            for sk, val in need.items():
                if waited.get(sk, 0) >= val:
                    continue
                eng.wait_ge(sems[sk], val)
                waited[sk] = val
            if op["fn"] is None:
                continue
            ins = op["fn"](eng)
            tok = op["tok"]
            if tok is not None:
                ins.then_inc(sems[tok[0]], 1 if e in COMPUTE else 16)

    def sem_keys(self):
        keys = list(COMPUTE)
        for q in self.dma_queues:
            keys += [(q, j) for j in range(NDMA)]
        return keys


def build(NB, SEQ):
    NT = SEQ // 128
    NS = SEQ // 512
    NB1 = NB + 1
    NG = NB * NS
    nc = bass.Bass("TRN2", target_bir_lowering=False)

    def dt(name, shape, kind="ExternalInput"):
        return nc.dram_tensor(name, shape, F32, kind=kind).ap()

    x_d = dt("x", [NB * SEQ, D])
    ctx_d = dt("ctx", [NB * CTXL, D])
    cT_d = dt("cT", [128, 8 * NB1])
    wada_d = dt("w_ada", [D, 3 * D])
    bada_d = dt("b_adaT", [128, 24])
    win_d = dt("w_in", [D, INW])
    wout_d = dt("w_out", [D, D])
    sink_d = dt("sinkp", [1, 8])
    cw_d = dt("conv_wT", [128, 124])
    cvec_d = dt("cvec", [128, 12])
    pg_d = dt("post_g", [1, D])
    pb_d = dt("post_b", [1, D])
    cos_d = dt("cosT", [128, SEQ])
    sin_d = dt("sinT", [128, SEQ])
    cm_d = dt("cmat", [128, 512])
    out_d = dt("out", [NB * SEQ, D], "ExternalOutput")
    diag_d = nc.dram_tensor("diag_scr", [128, 4, 32 * 128], BF16, kind="Internal").ap()

    S = Sched()
    st = contextlib.ExitStack()

    def sb(name, shape, dtype=F32):
        return st.enter_context(nc.sbuf_tensor(name, shape, dtype))

    w_in_bf = sb("w_in_bf", [128, 8, INW], BF16)
    w_out_bf = sb("w_out_bf", [128, 8, D], BF16)
    stage = sb("stage", [128, 5, 1024])
    identf = sb("identf", [128, 128])
    onesf = sb("onesf", [128, 128])
    cb16 = sb("cb16", [128, 4, 128], BF16)
    ident_bf, R_bf, maskA, maskB = (cb16[:, i, :] for i in range(4))
    ones512 = sb("ones512", [128, 128], BF16)
    cs = sb("cs", [128, 2, 2, 512])
    cT_sb = sb("cT_sb", [128, 8, NB1])
    scT = sb("scT", [128, 8, NB1])
    bada = sb("bada", [128, 24])
    modT = sb("modT", [128, 24, NB1])
    scale1T = sb("scale1T", [128, 8, NB1])
    gate_bc = sb("gate_bc", [128, D])
    g_bc = sb("g_bc", [128, D])
    b_bc = sb("b_bc", [128, D])
    hT = sb("hT", [128, 2, 8, 512], BF16)
    kT = sb("kT", [128, SEQ + CTXL], BF16)
    Vaug = sb("Vaug", [128, NT + 2, 192], BF16)
    uT = sb("uT", [128, 4, SEQ + 30], BF16)
    qT = sb("qT", [128, 4, 512], BF16)
    xn_bf = sb("xn_bf", [128, 2, D], BF16)
    NPT = 4
    PT = sb("PT", [128, NPT, 512], BF16)
    ropeb = sb("ropeb", [128, 512], BF16)
    mixA = sb("mixA", [128, 2, 4, 512], BF16)
    mixC = sb("mixC", [128, 4, 512], BF16)
    NTF = 4
    TF = sb("TF", [128, NTF, 512])
    sag2 = sb("sag2", [128, 4, 512], BF16)
    scg2 = sb("scg2", [128, 4, 512], BF16)
    convf = sb("convf", [128, 4, 512])
    conv_bf = sb("conv_bf", [128, 4, 512], BF16)
    NDG = 3
    diag = sb("diag", [128, NDG, 4, 128], BF16)
    numA = sb("numA", [128, 512])
    denA = sb("denA", [128, 512])
    cvec = sb("cvec_sb", [128, 12])
    stats = sb("stats", [128, 4, 2, 6])
    mv = sb("mv", [128, 4, 2])
    rs = sb("rs", [128, 4, 2])
    expo = sb("expo", [128, 4])
    st8 = sb("st8", [128, 20])
    sinkrow = sb("sinkrow", [1, 8])
    esr = sb("esr", [1, 8])
    esrow = sb("esrow", [1, 8], BF16)
    sinkL = sb("sinkL", [1, 2, 128], BF16)
    ps = [st.enter_context(nc.psum_tensor(f"ps{i}", [128, 512], F32)) for i in range(8)]
    PB = [0, 1, 2, 3]
    TM = 3
    SB_ = [4, 5]
    OB = [6, 7]
    pcnt = [0]

    def next_p():
        b = PB[pcnt[0] % len(PB)]
        pcnt[0] += 1
        return b

    tfc = [0]

    def next_tf():
        i = tfc[0] % NTF
        tfc[0] += 1
        return i

    cwh = TF[:, 3, 0:124]

    def xin(sl):
        return stage[:, sl, :]

    xres = stage[:, 2, :]

    def Rt(sl):
        return stage[:, 3 + sl, :]

    S.add("sp", lambda e: e.dma_start(out=identf[:], in_=cm_d[:, 0:128]), w=["identf"])
    S.add("sp", lambda e: e.dma_start(out=TF[:, 0, :], in_=cm_d), w=[("tf", 0)])
    S.add("dve", lambda e: e.tensor_copy(out=cb16[:].rearrange("p a b -> p (a b)"), in_=TF[:, 0, :]),
          r=[("tf", 0)], w=["cb16"])
    MASK_PE = True
    if MASK_PE:
      S.add("dve", lambda e: e.tensor_scalar(out=cb16[:, 2:4, :].rearrange("p a b -> p (a b)"),
                                           in0=cb16[:, 2:4, :].rearrange("p a b -> p (a b)"), scalar1=-1.0, scalar2=30000.0,
                                           op0=ALU.add, op1=ALU.mult), r=["cb16"], w=["cb16"])
    S.add("pool", lambda e: e.memset(ones512[:], 1.0 / 512.0), w=["ones512"])
    S.add("pool", lambda e: e.memset(expo[:], -0.5), w=["expo"])
    S.add("pool", lambda e: e.memset(onesf[:], 1.0), w=["onesf"])
    S.add("pool", lambda e: e.memset(Vaug[:].rearrange("p a b -> p (a b)"), 1.0), w=[("V", t) for t in range(NT + 2)])
    S.add("pool", lambda e: e.memset(uT[:].rearrange("p a b -> p (a b)"), 0.0), w=[("u", s) for s in range(-1, NS + 1)])
    S.add("pool", lambda e: e.memset(sinkL[:].rearrange("p a b -> p (a b)"), 0.0), w=["sinkL"])

    def _sl(e):
        e.memset(sinkL[:, 0, 64:128], 1.0)
        return e.memset(sinkL[:, 1, 0:64], 1.0)
    S.add("pool", _sl, w=["sinkL"])
    S.add("sp", lambda e: e.dma_start(out=bada[:], in_=bada_d), w=["bada"])
    S.add("sp", lambda e: e.dma_start(out=cvec[:], in_=cvec_d), w=["cvec"])
    S.add("sp", lambda e: e.dma_start(out=cwh, in_=cw_d), w=[("tf", 3)])
    S.add("sp", lambda e: e.dma_start(out=sinkrow[:], in_=sink_d), w=["sinkrow"])
    S.add("sp", lambda e: e.dma_start(out=g_bc[:], in_=pg_d.partition_broadcast(128)), w=["g_bc"])
    S.add("sp", lambda e: e.dma_start(out=b_bc[:], in_=pb_d.partition_broadcast(128)), w=["b_bc"])
    S.add("sp", lambda e: e.dma_start(out=cT_sb[:].rearrange("p a b -> p (a b)"), in_=cT_d), w=["cT"])
    S.add("dve", lambda e: e.tensor_scalar(out=cwh, in0=cwh, scalar1=0.5, scalar2=None, op0=ALU.mult),
          r=[("tf", 3)], w=[("tf", 3)])
    S.add("act", lambda e: e.activation(out=esr[:], in_=sinkrow[:], func=AF.Exp), r=["sinkrow"], w=["esr"])

    S.add("dve", lambda e: e.tensor_copy(out=esrow[:], in_=esr[:]), r=["esr"], w=["esrow"])
    cflat = cT_sb[:].rearrange("p a b -> p (a b)")
    sflat = scT[:].rearrange("p a b -> p (a b)")
    S.add("act", lambda e: e.activation(out=sflat, in_=cflat, func=AF.Tanh, scale=0.5), r=["cT"], w=["scT"])
    S.add("dve", lambda e: e.scalar_tensor_tensor(out=sflat, in0=sflat, scalar=1.0, in1=cflat, op0=ALU.add,
                                                  op1=ALU.mult), r=["scT", "cT"], w=["scT"])
    S.add("dve", lambda e: e.tensor_scalar(out=sflat, in0=sflat, scalar1=0.5, scalar2=None, op0=ALU.mult),
          r=["scT"], w=["scT"])

    stg_regions = [[("stg", 0), ("stg", 1)], [("stg", 2), ("stg", 3)]]

    def stg_slot(sl):
        return stage[:, 2 * sl:2 * sl + 2, :].rearrange("p a b -> p (a b)")

    wada_v = wada_d.rearrange("(kc p) n -> p kc n", p=128)
    wout_v = wout_d.rearrange("(kc p) n -> p kc n", p=128)
    modps = ps[TM][:, 0:24 * NB1].rearrange("p (a b) -> p a b", b=NB1)
    wslot = [TF[:, 0:3, :].rearrange("p a b -> p (a b)"), convf[:].rearrange("p a b -> p (a b)")]
    wslot_regions = [[("tf", 0), ("tf", 1), ("tf", 2)], [("convf", c) for c in range(4)]]
    HW = INW // 2
    wtasks = [("in", kc, part) for kc in range(8) for part in range(2)] + [("out", kc, 0) for kc in range(8)]
    wi = [0]

    def emit_wtask():
        if wi[0] >= len(wtasks):
            return
        idx = wi[0]
        wi[0] += 1
        kind, kc, part = wtasks[idx]
        sl = idx % 2
        regs = wslot_regions[sl]
        eng = "dve" if idx % 2 == 0 else "act"
        if kind == "in":
            c0 = part * HW
            sv = wslot[sl][:, 0:HW]
            src = win_d[kc * 128:(kc + 1) * 128, c0:c0 + HW]
            dst = w_in_bf[:, kc, c0:c0 + HW]
            wreg = [("w_in", kc, part)]
        else:
            sv = wslot[sl][:, 0:1024]
            src = wout_d[kc * 128:(kc + 1) * 128, :]
            dst = w_out_bf[:, kc, :]
            wreg = [("w_out", kc)]
        S.add("sp", lambda e: e.dma_start(out=sv, in_=src), w=regs)
        if eng == "act":
            S.add("act", lambda e: e.activation(out=dst, in_=sv, func=AF.Copy), r=regs, w=wreg)
        else:
            S.add("dve", lambda e: e.tensor_copy(out=dst, in_=sv), r=regs, w=wreg)

    for blk in range(12):
        sl = blk % 2
        sv = stg_slot(sl).rearrange("p (kc n) -> p kc n", n=256)

        def _ld(e, sv=sv, blk=blk):
            return e.dma_start(out=sv, in_=wada_v[:, :, blk * 256:(blk + 1) * 256])
        S.add("sp", _ld, w=stg_regions[sl])

        def _mm(e, sv=sv, blk=blk):
            for nn in range(2):
                n = blk * 2 + nn
                for kc in range(8):
                    ins = e.matmul(modps[:, n, :], lhsT=sv[:, kc, nn * 128:(nn + 1) * 128], rhs=scT[:, kc, :],
                                   start=(kc == 0), stop=(kc == 7))
            return ins
        S.add("pe", _mm, r=stg_regions[sl] + ["scT"], w=[("ps", TM)])
        emit_wtask()
        emit_wtask()
    while wi[0] < len(wtasks):
        emit_wtask()

    def _modev(e):
        for n in range(24):
            ins = e.activation(out=modT[:, n, :], in_=modps[:, n, :], func=AF.Identity, bias=bada[:, n:n + 1])
        return ins
    S.add("act", _modev, r=[("ps", TM), "bada"], w=["modT"])
    S.add("dve", lambda e: e.tensor_scalar(out=scale1T[:].rearrange("p a b -> p (a b)"),
                                           in0=modT[:, 8:16, :].rearrange("p a b -> p (a b)"), scalar1=1.0,
                                           scalar2=None, op0=ALU.add), r=["modT"], w=["scale1T"])
    W_IN_R = [("w_in", kc, p) for kc in range(8) for p in range(2)]
    W_OUT_R = [("w_out", kc) for kc in range(8)]
    stg_bf = stage[:, 0:4, :].rearrange("p a b -> p (a b)").bitcast(BF16)
    identb31 = identf[:].unsqueeze(1).to_broadcast([128, 31, 128])
    for c in range(4):
        sl = c % 2
        dv = stg_bf[:, sl * 4096:sl * 4096 + 31 * 128]
        eng = "dve" if c % 2 == 0 else "pool"
        S.add(eng, lambda e, dv=dv, c=c: e.tensor_tensor(
            out=dv.rearrange("p (a b) -> p a b", b=128), in0=identb31,
            in1=cwh[:, c * 31:(c + 1) * 31].unsqueeze(2).to_broadcast([128, 31, 128]), op=ALU.mult),
            r=["identf", ("tf", 3)], w=stg_regions[sl])
        S.add("sp", lambda e, dv=dv, c=c: e.dma_start(out=diag_d[:, c, 0:31 * 128], in_=dv), r=stg_regions[sl],
              w=[("diag_d", c)])

    lncnt = [0, 0]

    def layernorm_stats(src_ap, src_regions, width=1024, pool=0):
        sl = 2 * pool + lncnt[pool] % 2
        lncnt[pool] += 1
        nch = width // 512

        def _bn(e):
            for c in range(nch):
                ins = e.bn_stats(out=stats[:, sl, c, :], in_=src_ap[:, c * 512:(c + 1) * 512])
            return ins
        S.add("dve", _bn, r=src_regions, w=[("stats", sl)])
        S.add("dve", lambda e: e.bn_aggr(out=mv[:, sl, :], in_=stats[:, sl, 0:nch, :]), r=[("stats", sl)],
              w=[("mv", sl)])
        S.add("dve", lambda e: e.tensor_scalar(out=rs[:, sl, 0:1], in0=mv[:, sl, 1:2], scalar1=LN_EPS, scalar2=None,
                                               op0=ALU.add), r=[("mv", sl)], w=[("rs", sl)])
        S.add("pool", lambda e: e.tensor_tensor(out=rs[:, sl, 0:1], in0=rs[:, sl, 0:1], in1=expo[:, 0:1], op=ALU.pow),
              r=[("rs", sl), "expo"], w=[("rs", sl)])
        return sl

    xcnt = [0]

    def ln_part(src_dram_rows):
        xs = xcnt[0] % 2
        xcnt[0] += 1
        S.add("sp", lambda e: e.dma_start(out=xin(xs), in_=src_dram_rows), w=[("stg", xs)])
        sl = layernorm_stats(xin(xs), [("stg", xs)])
        S.add("pool", lambda e: e.tensor_scalar(out=rs[:, sl, 1:2], in0=mv[:, sl, 0:1], scalar1=-1.0,
                                                scalar2=rs[:, sl, 0:1], op0=ALU.mult, op1=ALU.mult),
              r=[("mv", sl), ("rs", sl)], w=[("rs2", sl)])
        S.add("pool", lambda e: e.tensor_scalar(out=xn_bf[:, xs, :], in0=xin(xs), scalar1=rs[:, sl, 0:1],
                                                scalar2=rs[:, sl, 1:2], op0=ALU.mult, op1=ALU.add),
              r=[("stg", xs), ("rs", sl), ("rs2", sl)], w=[("xn", xs)])
        return xs

    def tr_part(xs, bcol, dst_hT_ap, dst_regions):
        tb0 = next_p()
        tb1 = next_p()
        tp0 = ps[tb0][:].bitcast(BF16).rearrange("p (a b) -> p a b", b=128)
        tp1 = ps[tb1][:].bitcast(BF16).rearrange("p (a b) -> p a b", b=128)

        def _tr0(e):
            for kc in range(4):
                ins = e.transpose(out=tp0[:, kc, :], in_=xn_bf[:, xs, kc * 128:(kc + 1) * 128], identity=ident_bf)
            return ins
        S.add("pe", _tr0, r=[("xn", xs), "cb16"], w=[("ps", tb0)])

        def _tr1(e):
            for kc in range(4, 8):
                ins = e.transpose(out=tp1[:, kc - 4, :], in_=xn_bf[:, xs, kc * 128:(kc + 1) * 128], identity=ident_bf)
            return ins
        S.add("pe", _tr1, r=[("xn", xs), "cb16"], w=[("ps", tb1)])

        def _ev0(e):
            for kc in range(4):
                ins = e.activation(out=dst_hT_ap[:, kc, :], in_=tp0[:, kc, :], func=AF.Identity,
                                   scale=scale1T[:, kc, bcol:bcol + 1], bias=modT[:, kc, bcol:bcol + 1])
            return ins
        S.add("act", _ev0, r=[("ps", tb0), "scale1T", "modT"], w=[dst_regions[0]])

        def _ev1(e):
            for kc in range(4, 8):
                ins = e.tensor_scalar(out=dst_hT_ap[:, kc, :], in0=tp1[:, kc - 4, :], scalar1=scale1T[:, kc, bcol:bcol + 1],
                                      scalar2=modT[:, kc, bcol:bcol + 1], op0=ALU.mult, op1=ALU.add)
            return ins
        S.add("dve", _ev1, r=[("ps", tb1), "scale1T", "modT"], w=[dst_regions[1]])

    def HTR(hs):
        return [(("hT", hs), "lo"), (("hT", hs), "hi")]

    def proj_fm(col0, rhs_ap, rhs_regions, n):
        b = next_p()

        def _mm(e):
            for kc in range(8):
                ins = e.matmul(ps[b][:, 0:n], lhsT=w_in_bf[:, kc, col0:col0 + 128], rhs=rhs_ap[:, kc, :],
                               start=(kc == 0), stop=(kc == 7))
            return ins
        S.add("pe", _mm, r=W_IN_R + list(rhs_regions), w=[("ps", b)])
        return b

    def rope(b, csl, dst_ap, dst_regions):
        t1 = next_tf()
        t2 = next_tf()
        rb = next_p()
        S.add("act", lambda e: e.activation(out=ropeb[:], in_=ps[b][:], func=AF.Copy), r=[("ps", b)], w=["ropeb"])
        S.add("pe", lambda e: e.matmul(ps[rb][:], lhsT=R_bf, rhs=ropeb[:], start=True, stop=True),
              r=["ropeb", "cb16"], w=[("ps", rb)])
        S.add("dve", lambda e: e.tensor_tensor(out=TF[:, t1, :], in0=ps[b][:], in1=cs[:, csl, 0, :], op=ALU.mult),
              r=[("ps", b), ("cs", csl)], w=[("tf", t1)])
        S.add("dve", lambda e: e.tensor_tensor(out=TF[:, t2, :], in0=ps[rb][:], in1=cs[:, csl, 1, :], op=ALU.mult),
              r=[("ps", rb), ("cs", csl)], w=[("tf", t2)])
        S.add("pool", lambda e: e.tensor_tensor(out=dst_ap, in0=TF[:, t1, :], in1=TF[:, t2, :], op=ALU.add),
              r=[("tf", t1), ("tf", t2)], w=dst_regions)

    def gated_tanh(b, dst_ap, dst_regions):
        t = next_tf()
        S.add("act", lambda e: e.activation(out=TF[:, t, :], in_=ps[b][:], func=AF.Tanh, scale=0.5),
              r=[("ps", b)], w=[("tf", t)])
        S.add("dve", lambda e: e.scalar_tensor_tensor(out=dst_ap, in0=TF[:, t, :], scalar=1.0, in1=ps[b][:],
                                                      op0=ALU.add, op1=ALU.mult),
              r=[("tf", t), ("ps", b)], w=dst_regions)

    def stage_ctx(b, hs):
        hc = hT[:, hs, :, 0:CTXL]
        xs0 = ln_part(ctx_d[b * CTXL:b * CTXL + 128, :])
        xs1 = ln_part(ctx_d[b * CTXL + 128:b * CTXL + 256, :])
        tr_part(xs0, NB, hT[:, hs, :, 0:128], HTR(hs))
        tr_part(xs1, NB, hT[:, hs, :, 128:256], HTR(hs))
        pb = proj_fm(KO, hc, HTR(hs), CTXL)
        S.add("act", lambda e: e.activation(out=kT[:, SEQ:SEQ + CTXL], in_=ps[pb][:, 0:CTXL], func=AF.Copy),
              r=[("ps", pb)], w=[("k", NT), ("k", NT + 1)])
        vb = next_p()
        vps = ps[vb][:, 0:256].rearrange("p (a b) -> p a b", b=128)

        def _mm(e):
            for ct in range(2):
                for kc in range(8):
                    ins = e.matmul(vps[:, ct, :], lhsT=hT[:, hs, kc, ct * 128:(ct + 1) * 128],
                                   rhs=w_in_bf[:, kc, VO:VO + 128], start=(kc == 0), stop=(kc == 7))
            return ins
        S.add("pe", _mm, r=W_IN_R + HTR(hs), w=[("ps", vb)])

        def _ev(e):
            e.tensor_copy(out=Vaug[:, NT:NT + 2, 0:64], in_=vps[:, :, 0:64])
            return e.tensor_copy(out=Vaug[:, NT:NT + 2, 128:192], in_=vps[:, :, 64:128])
        S.add("dve", _ev, r=[("ps", vb)], w=[("V", NT), ("V", NT + 1)])

    def gate_broadcast(b):
        for half in range(2):
            gb = next_p()
            for c4 in range(4):
                kc = half * 4 + c4
                t = next_tf()
                S.add("dve", lambda e, kc=kc, t=t: e.tensor_copy(out=TF[:, t, 0:128],
                                                               in_=modT[:, 16 + kc, b:b + 1].to_broadcast([128, 128])),
                      r=["modT"], w=[("tf", t)])
                S.add("pe", lambda e, c4=c4, gb=gb, t=t: e.matmul(ps[gb][:, c4 * 128:(c4 + 1) * 128], lhsT=TF[:, t, 0:128],
                                                                 rhs=identf[:], start=True, stop=True),
                      r=[("tf", t), "identf"], w=[("ps", gb)])
            S.add("act", lambda e, gb=gb, half=half: e.activation(out=gate_bc[:, half * 512:(half + 1) * 512], in_=ps[gb][:],
                                                                func=AF.Copy), r=[("ps", gb)], w=["gate_bc"])

    prefetched = {}

    def stream_A(G):
        b, s = divmod(G, NS)
        hs = G % 2
        csl = G % 2
        S.add("sp", lambda e: e.dma_start(out=cs[:, csl, 0, :], in_=cos_d[:, s * 512:(s + 1) * 512]), w=[("cs", csl)])
        S.add("sp", lambda e: e.dma_start(out=cs[:, csl, 1, :], in_=sin_d[:, s * 512:(s + 1) * 512]), w=[("cs", csl)])
        r0 = b * SEQ + s * 512
        if G in prefetched:
            xsl = [prefetched.pop(G)]
        else:
            xsl = [ln_part(x_d[r0:r0 + 128, :])]
            yield
        for i in range(4):
            if i < 3:
                xsl.append(ln_part(x_d[r0 + (i + 1) * 128:r0 + (i + 2) * 128, :]))
            tr_part(xsl[i], b, hT[:, hs, :, i * 128:(i + 1) * 128], HTR(hs))
            yield
        hv = hT[:, hs, :, :]
        pb = proj_fm(KO, hv, HTR(hs), 512)
        rope(pb, csl, kT[:, s * 512:(s + 1) * 512], [("k", 4 * s + i) for i in range(4)])
        yield
        vb = next_p()
        vps = ps[vb][:].rearrange("p (a b) -> p a b", b=128)

        def _mm(e):
            for i in range(4):
                for kc in range(8):
                    ins = e.matmul(vps[:, i, :], lhsT=hT[:, hs, kc, i * 128:(i + 1) * 128],
                                   rhs=w_in_bf[:, kc, VO:VO + 128], start=(kc == 0), stop=(kc == 7))
            return ins
        S.add("pe", _mm, r=W_IN_R + HTR(hs), w=[("ps", vb)])

        def _ev(e):
            e.tensor_copy(out=Vaug[:, 4 * s:4 * s + 4, 0:64], in_=vps[:, :, 0:64])
            return e.tensor_copy(out=Vaug[:, 4 * s:4 * s + 4, 128:192], in_=vps[:, :, 64:128])
        S.add("dve", _ev, r=[("ps", vb)], w=[("V", 4 * s + i) for i in range(4)])
        yield
        for c in range(4):
            pcb = proj_fm(CBO + c * 128, hv, HTR(hs), 512)
            t = next_tf()
            S.add("act", lambda e, pcb=pcb, t=t: e.activation(out=TF[:, t, :], in_=ps[pcb][:], func=AF.Tanh, scale=0.5),
                  r=[("ps", pcb)], w=[("tf", t)])
            pca = proj_fm(CAO + c * 128, hv, HTR(hs), 512)
            S.add("dve", lambda e, pca=pca, t=t, c=c: e.scalar_tensor_tensor(
                out=uT[:, c, 15 + s * 512:15 + (s + 1) * 512], in0=TF[:, t, :], scalar=1.0, in1=ps[pca][:],
                op0=ALU.add, op1=ALU.mult), r=[("tf", t), ("ps", pca)], w=[("u", s)])
            yield
        if G + 1 < NG and not (G % NS == 0 and G > 0):
            b2, s2 = divmod(G + 1, NS)
            r2 = b2 * SEQ + s2 * 512
            prefetched[G + 1] = ln_part(x_d[r2:r2 + 128, :])
            yield

    ptc = [0]
    KHY = 1

    def stream_H(G):
        b, s = divmod(G, NS)
        hs = G % 2
        csl = G % 2
        ms = G % 2
        hv = hT[:, hs, :, :]
        for j in range(4):
            pb = proj_fm(QO + j * 128, hv, HTR(hs), 512)
            rope(pb, csl, qT[:, j, :], ["q"])
            yield
        for j in range(4):
            pb = proj_fm(AGO + j * 128, hv, HTR(hs), 512)
            gated_tanh(pb, sag2[:, j, :], [("sag2", j)])
            yield
        for i in range(4):
            t = 4 * s + i
            kts = [(t, None), (NT, None), (NT + 1, None)]
            if t > 0:
                kts.append((t - 1, maskA))
            if t < NT - 1:
                kts.append((t + 1, maskB))
                if i == 3:
                    yield "needA"
            if True:
                pairs = [(g, kt, m) for (kt, m) in kts for g in range(2)]
            else:
                pairs = [(g, kt, m) for g in range(2) for (kt, m) in kts]
            nk = len(kts)
            info = []

            def emit_S(n, i=i):
                g, kt, m = pairs[n]
                sbk = SB_[n % 2]
                def _smm(e, g=g, kt=kt, sbk=sbk, m=m, i=i):
                    o = ps[sbk][:].rearrange("p (a b) -> p a b", b=128)
                    ins = e.matmul(o, lhsT=kT[g * 64:(g + 1) * 64, kt * 128:(kt + 1) * 128],
                                   rhs=qT[g * 64:(g + 1) * 64, :, i * 128:(i + 1) * 128], start=True, stop=(m is None or not MASK_PE))
                    if m is not None and MASK_PE:
                        ins = e.matmul(o, lhsT=ident_bf, rhs=m.unsqueeze(1).to_broadcast([128, 4, 128]), start=False,
                                       stop=True)
                    return ins
                S.add("pe", _smm, r=[("k", kt), "q", "cb16"], w=[("ps", sbk)])
                p = ptc[0] % NPT
                ptc[0] += 1
                S.add("act", lambda e, sbk=sbk, p=p: e.activation(out=PT[:, p, :], in_=ps[sbk][:], func=AF.Exp,
                                                                  scale=0.125), r=[("ps", sbk)], w=[("pt", p)])
                if m is not None and not MASK_PE:
                    S.add("dve", lambda e, p=p, m=m: e.tensor_tensor(
                        out=PT[:, p, :].rearrange("p (a b) -> p a b", b=128),
                        in0=PT[:, p, :].rearrange("p (a b) -> p a b", b=128),
                        in1=m.unsqueeze(1).to_broadcast([128, 4, 128]), op=ALU.mult),
                        r=[("pt", p), "cb16"], w=[("pt", p)])
                info.append(p)

            def emit_PV(n):
                g, kt, m = pairs[n]
                p = info[n]
                first = (n < 2) if True else (n % nk == 0)
                S.add("pe", lambda e, g=g, kt=kt, p=p, first=first: e.matmul(
                    ps[OB[g]][:], lhsT=Vaug[:, kt, 64 * g:64 * g + 128], rhs=PT[:, p, :], start=first, stop=False),
                    r=[("V", kt), ("pt", p)], w=[("ps", OB[g])])
                if (n >= len(pairs) - 2) if True else (n % nk == nk - 1):
                    S.add("pe", lambda e, g=g: e.matmul(
                        ps[OB[g]][:].rearrange("p (a b) -> p a b", b=128), lhsT=sinkL[:, g, :],
                        rhs=esrow[:, 4 * g:4 * g + 4].unsqueeze(2).to_broadcast([1, 4, 128]), start=False, stop=True),
                        r=["sinkL", "esrow"], w=[("ps", OB[g])])

            emit_S(0)
            emit_S(1)
            for n in range(0, len(pairs), 2):
                if n + 2 < len(pairs):
                    emit_S(n + 2)
                    emit_S(n + 3)
                emit_PV(n)
                emit_PV(n + 1)
                if (n // 2) % KHY == KHY - 1:
                    yield
            S.add("act", lambda e: e.activation(out=denA[0:64, :], in_=ps[OB[0]][64:128, :], func=AF.Copy),
                  r=[("ps", OB[0])], w=["denA0"])
            S.add("act", lambda e: e.activation(out=denA[64:128, :], in_=ps[OB[1]][0:64, :], func=AF.Copy),
                  r=[("ps", OB[1])], w=["denA1"])
            S.add("act", lambda e: e.activation(out=numA[0:64, :], in_=ps[OB[0]][0:64, :], func=AF.Copy),
                  r=[("ps", OB[0])], w=["numA0"])
            S.add("act", lambda e: e.activation(out=numA[64:128, :], in_=ps[OB[1]][64:128, :], func=AF.Copy),
                  r=[("ps", OB[1])], w=["numA1"])
            S.add("dve", lambda e: e.reciprocal(out=denA[:], in_=denA[:]), r=["denA0", "denA1"], w=["denA0", "denA1"])
            ta = next_tf()
            S.add("dve", lambda e, ta=ta: e.scalar_tensor_tensor(out=TF[:, ta, :], in0=numA[:], scalar=0.5, in1=denA[:],
                                                                 op0=ALU.mult, op1=ALU.mult),
                  r=["numA0", "numA1", "denA0", "denA1"], w=[("tf", ta)])
            S.add("pool", lambda e, i=i, ta=ta: e.tensor_tensor(
                out=mixA[:, ms, :, i * 128:(i + 1) * 128], in0=TF[:, ta, :].rearrange("p (a b) -> p a b", b=128),
                in1=sag2[:, :, i * 128:(i + 1) * 128], op=ALU.mult),
                r=[("tf", ta)] + [("sag2", j) for j in range(4)], w=[("mixA", ms, i)])
            yield
        yield "needT"
        for c in range(4):
            pb = proj_fm(CGO + c * 128, hv, HTR(hs), 512)
            gated_tanh(pb, scg2[:, c, :], [("scg2", c)])
            yield

    dgc = [0]
    fcnt = [0]

    def stream_T(G):
        b, s = divmod(G, NS)
        ms = G % 2
        if s == 0:
            gate_broadcast(b)
            yield
        for c in range(4):
            cbk = next_p()
            for j0 in range(0, 31, 4):
                nj = min(4, 31 - j0)
                dg = dgc[0] % NDG
                dgc[0] += 1
                S.add("sp", lambda e, dg=dg, c=c, j0=j0, nj=nj: e.dma_start(
                    out=diag[:, dg, 0:nj, :].rearrange("p a b -> p (a b)"),
                    in_=diag_d[:, c, j0 * 128:(j0 + nj) * 128]), r=[("diag_d", c)], w=[("diag", dg)])

                def _cm(e, dg=dg, c=c, j0=j0, nj=nj, cbk=cbk):
                    for jj in range(nj):
                        j = j0 + jj
                        ins = e.matmul(ps[cbk][:], lhsT=diag[:, dg, jj, :], rhs=uT[:, c, s * 512 + j:s * 512 + j + 512],
                                       start=(j == 0), stop=(j == 30))
                    return ins
                S.add("pe", _cm, r=[("diag", dg), ("u", s - 1), ("u", s), ("u", s + 1)], w=[("ps", cbk)])
            S.add("act", lambda e, c=c, cbk=cbk: e.activation(out=convf[:, c, :], in_=ps[cbk][:], func=AF.Identity,
                                                              bias=cvec[:, c:c + 1]), r=[("ps", cbk), "cvec"],
                  w=[("convf", c)])
            S.add("act", lambda e, c=c, cbk=cbk: e.activation(out=mixC[:, c, :], in_=ps[cbk][:], func=AF.Square,
                                                              bias=cvec[:, c:c + 1]), r=[("ps", cbk), "cvec"],
                  w=[("mixC", c)])
            S.add("pool", lambda e, c=c: e.tensor_copy(out=conv_bf[:, c, :], in_=convf[:, c, :]), r=[("convf", c)],
                  w=[("cbf", c)])
            yield
        sbk = next_p()

        def _st(e):
            for a in range(4):
                for c in range(4):
                    e.matmul(ps[sbk][:, a:a + 1], lhsT=conv_bf[:, c, a * 128:(a + 1) * 128], rhs=ones512[:, 0:1],
                             start=(c == 0), stop=(c == 3))
            for a in range(4):
                for c in range(4):
                    ins = e.matmul(ps[sbk][:, 4 + a:5 + a], lhsT=mixC[:, c, a * 128:(a + 1) * 128],
                                   rhs=ones512[:, 0:1], start=(c == 0), stop=(c == 3))
            return ins
        S.add("pe", _st, r=["ones512"] + [("cbf", c) for c in range(4)] + [("mixC", c) for c in range(4)],
              w=[("ps", sbk)])
        S.add("dve", lambda e: e.tensor_copy(out=st8[:, 0:8], in_=ps[sbk][:, 0:8]), r=[("ps", sbk)], w=["st8a"])
        S.add("dve", lambda e: e.tensor_tensor(out=st8[:, 8:12], in0=st8[:, 0:4], in1=st8[:, 0:4], op=ALU.mult),
              r=["st8a"], w=["st8b"])
        S.add("dve", lambda e: e.scalar_tensor_tensor(out=st8[:, 12:16], in0=st8[:, 4:8], scalar=LN_EPS, in1=st8[:, 8:12],
                                                      op0=ALU.add, op1=ALU.subtract), r=["st8a", "st8b"], w=["st8c"])
        S.add("pool", lambda e: e.tensor_tensor(out=st8[:, 12:16], in0=st8[:, 12:16], in1=expo[:, 0:4], op=ALU.pow),
              r=["st8c", "expo"], w=["st8c"])
        S.add("dve", lambda e: e.scalar_tensor_tensor(out=st8[:, 16:20], in0=st8[:, 0:4], scalar=-1.0, in1=st8[:, 12:16],
                                                      op0=ALU.mult, op1=ALU.mult), r=["st8a", "st8c"], w=["st8d"])
        tm = next_tf()
        tv = next_tf()
        identb = identf[:].unsqueeze(1).to_broadcast([128, 4, 128])
        S.add("dve", lambda e: e.tensor_tensor(out=TF[:, tm, :].rearrange("p (a b) -> p a b", b=128), in0=identb,
                                               in1=st8[:, 12:16].unsqueeze(2).to_broadcast([128, 4, 128]), op=ALU.mult),
              r=["identf", "st8c"], w=[("tf", tm)])
        S.add("dve", lambda e: e.tensor_tensor(out=TF[:, tv, :].rearrange("p (a b) -> p a b", b=128), in0=identb,
                                               in1=st8[:, 16:20].unsqueeze(2).to_broadcast([128, 4, 128]), op=ALU.mult),
              r=["identf", "st8d"], w=[("tf", tv)])
        rbk = next_p()
        nbk = next_p()
        S.add("pe", lambda e: e.matmul(ps[rbk][:], lhsT=onesf[:], rhs=TF[:, tm, :], start=True, stop=True),
              r=["onesf", ("tf", tm)], w=[("ps", rbk)])
        S.add("pe", lambda e: e.matmul(ps[nbk][:], lhsT=onesf[:], rhs=TF[:, tv, :], start=True, stop=True),
              r=["onesf", ("tf", tv)], w=[("ps", nbk)])
        def emit_z(c):
            S.add("dve", lambda e: e.tensor_scalar(out=convf[:, c, :], in0=convf[:, c, :], scalar1=cvec[:, 4 + c:5 + c],
                                                   scalar2=cvec[:, 8 + c:9 + c], op0=ALU.mult, op1=ALU.add),
                  r=[("convf", c), "cvec"], w=[("convf", c)])

        for c in range(4):
            S.add("dve", lambda e, c=c: e.tensor_tensor(out=convf[:, c, :], in0=convf[:, c, :], in1=ps[rbk][:],
                                                        op=ALU.mult), r=[("convf", c), ("ps", rbk)],
                  w=[("convf", c)])
            S.add("dve", lambda e, c=c: e.tensor_tensor(out=convf[:, c, :], in0=convf[:, c, :], in1=ps[nbk][:],
                                                        op=ALU.add), r=[("convf", c), ("ps", nbk)], w=[("convf", c)])
        emit_z(0)
        emit_z(1)
        yield
        for c in range(4):
            tz = next_tf()
            S.add("act", lambda e, c=c, tz=tz: e.activation(out=TF[:, tz, :], in_=convf[:, c, :], func=AF.Tanh, scale=0.5),
                  r=[("convf", c)], w=[("tf", tz)])
            if c + 2 < 4:
                emit_z(c + 2)
            S.add("dve", lambda e, c=c, tz=tz: e.scalar_tensor_tensor(out=TF[:, tz, :], in0=TF[:, tz, :], scalar=1.0,
                                                                      in1=convf[:, c, :], op0=ALU.add, op1=ALU.mult),
                  r=[("tf", tz), ("convf", c)], w=[("tf", tz)])
            S.add("dve", lambda e, c=c, tz=tz: e.scalar_tensor_tensor(out=mixC[:, c, :], in0=TF[:, tz, :], scalar=0.25,
                                                                      in1=scg2[:, c, :], op0=ALU.mult, op1=ALU.mult),
                  r=[("tf", tz), ("scg2", c)], w=[("mixC", c)])
            yield
        for i in range(4):
            fs = fcnt[0] % 2
            fcnt[0] += 1
            r0 = b * SEQ + s * 512 + i * 128
            S.add("sp", lambda e, r0=r0: e.dma_start(out=xres, in_=x_d[r0:r0 + 128, :]), w=[("stg", 2)])
            for half in range(2):
                yb = next_p()

                def _mm(e, yb=yb, half=half, i=i):
                    for kc in range(4):
                        e.matmul(ps[yb][:], lhsT=mixA[:, ms, kc, i * 128:(i + 1) * 128],
                                 rhs=w_out_bf[:, kc, half * 512:(half + 1) * 512], start=(kc == 0), stop=False)
                    for kc in range(4):
                        ins = e.matmul(ps[yb][:], lhsT=mixC[:, kc, i * 128:(i + 1) * 128],
                                       rhs=w_out_bf[:, 4 + kc, half * 512:(half + 1) * 512], start=False, stop=(kc == 3))
                    return ins
                S.add("pe", _mm, r=W_OUT_R + [("mixA", ms, i)] + [("mixC", c) for c in range(4)], w=[("ps", yb)])
                S.add("dve", lambda e, yb=yb, half=half, fs=fs: e.tensor_tensor(
                    out=Rt(fs)[:, half * 512:(half + 1) * 512], in0=ps[yb][:], in1=gate_bc[:, half * 512:(half + 1) * 512],
                    op=ALU.mult), r=[("ps", yb), "gate_bc"], w=[("stg", 3 + fs)])
            yield
            S.add("dve", lambda e, fs=fs: e.scalar_tensor_tensor(out=Rt(fs), in0=xres, scalar=ALPHA, in1=Rt(fs),
                                                                 op0=ALU.mult, op1=ALU.add),
                  r=[("stg", 2), ("stg", 3 + fs)], w=[("stg", 3 + fs)])
            sl = layernorm_stats(Rt(fs), [("stg", 3 + fs)], pool=1)
            yield
            S.add("dve", lambda e, sl=sl, fs=fs: e.tensor_scalar(out=Rt(fs), in0=Rt(fs), scalar1=mv[:, sl, 0:1],
                                                                 scalar2=rs[:, sl, 0:1], op0=ALU.subtract, op1=ALU.mult),
                  r=[("stg", 3 + fs), ("mv", sl), ("rs", sl)], w=[("stg", 3 + fs)])
            S.add("pool", lambda e, fs=fs: e.tensor_tensor(out=Rt(fs), in0=Rt(fs), in1=g_bc[:], op=ALU.mult),
                  r=[("stg", 3 + fs), "g_bc"], w=[("stg", 3 + fs)])
            S.add("pool", lambda e, fs=fs: e.tensor_tensor(out=Rt(fs), in0=Rt(fs), in1=b_bc[:], op=ALU.add),
                  r=[("stg", 3 + fs), "b_bc"], w=[("stg", 3 + fs)])
            S.add("sp", lambda e, fs=fs, r0=r0: e.dma_start(out=out_d[r0:r0 + 128, :], in_=Rt(fs)),
                  r=[("stg", 3 + fs)], w=[("out", r0)])
            yield

    def drain(g):
        if g is None:
            return
        for _ in g:
            pass

    stage_ctx(0, 1)
    for G in range(NG + 2):
        gA = stream_A(G) if G < NG else None
        gH = stream_H(G - 1) if 1 <= G <= NG else None
        gT = stream_T(G - 2) if 2 <= G <= NG + 1 else None
        live = {"A": gA, "H": gH, "T": gT}
        order = ("H", "A", "T")
        while any(v is not None for v in live.values()):
            for name in order:
                g = live[name]
                if g is None:
                    continue
                try:
                    v = next(g)
                except StopIteration:
                    live[name] = None
                    continue
                if v == "needA":
                    drain(live["A"])
                    live["A"] = None
                elif v == "needT":
                    drain(live["T"])
                    live["T"] = None
        if G < NG and G > 0 and G % NS == 0:
            stage_ctx(G // NS, (G + 1) % 2)

    out_regions = [("out", b * SEQ + t * 128) for b in range(NB) for t in range(NT)]
    S.add("sp", None, r=out_regions)

    S.analyse()
    sems = {}
    for k in S.sem_keys():
        nm = "s_" + (k if isinstance(k, str) else f"{k[0]}{k[1]}")
        sems[k] = st.enter_context(nc.semaphore(nm))
    with nc.Block() as block:
        @block.tensor
        def _(e):
            S.emit_engine("pe", e, sems)

        @block.scalar
        def _(e):
            S.emit_engine("act", e, sems)

        @block.vector
        def _(e):
            S.emit_engine("dve", e, sems)

        @block.gpsimd
        def _(e):
            S.emit_engine("pool", e, sems)

        @block.sync
        def _(e):
            S.emit_engine("sp", e, sems)
    st.close()
    return nc


def _head_perm():
    idx = []
    for j in range(4):
        idx += list(range(j * 64, (j + 1) * 64)) + list(range((4 + j) * 64, (5 + j) * 64))
    return np.array(idx)


def _rope_tables(seq):
    grid_w = 64
    rows = seq // grid_w
    row = np.repeat(np.arange(rows, dtype=np.float32), grid_w)
    col = np.tile(np.arange(grid_w, dtype=np.float32), rows)
    inv_freq = (np.float32(10000.0) ** (-np.arange(0, 32, 2, dtype=np.float32) / np.float32(32))).astype(np.float32)
    cosT = np.zeros((128, seq), np.float32)
    sinT = np.zeros((128, seq), np.float32)
    for p in range(128):
        d = p % 64
        pos = row if d < 32 else col
        i = d % 32
        ang = (pos * inv_freq[i % 16]).astype(np.float32)
        cosT[p] = np.cos(ang)
        sinT[p] = np.sin(ang) * (-1.0 if i < 16 else 1.0)
    return cosT, sinT


def _const_mats():
    ident = np.eye(128, dtype=np.float32)
    R = np.zeros((128, 128), np.float32)
    for m in range(128):
        partner = m + 16 if (m % 32) < 16 else m - 16
        R[partner, m] = 1.0
    k = np.arange(128)[:, None]
    q = np.arange(128)[None, :]
    maskA = (k >= q).astype(np.float32)
    maskB = (k <= q).astype(np.float32)
    return np.concatenate([ident, R, maskA, maskB], axis=1)


def make_in_maps(inputs, ncores, NB, SEQ):
    f = lambda a: np.ascontiguousarray(np.asarray(a, dtype=np.float32))
    x = f(inputs["x"])
    c = f(inputs["c"])
    ctx = f(inputs["ctx"])
    c_ctx = f(inputs["c_ctx"])
    w_ada = f(inputs["w_ada"])[0]
    b_ada = f(inputs["b_ada"])[0]
    w_in = f(inputs["w_in"])[0]
    w_out = f(inputs["w_out"])[0]
    sink = f(inputs["attn_sink"])[0]
    conv_w = f(inputs["conv_w"])[0]
    hp = _head_perm()
    cols = np.concatenate([hp, 512 + np.arange(256), 768 + hp, 1280 + np.arange(1536)])
    w_in_p = f(w_in[:, cols])
    rows = np.concatenate([hp, 512 + np.arange(512)])
    w_out_p = f(w_out[rows, :])
    b_adaT = f(b_ada.reshape(24, 128).T)
    sinkp = f(np.concatenate([sink[0:4], sink[4:8]])[None, :])
    conv_wT = f(conv_w.T.reshape(4, 128, 31).transpose(1, 0, 2).reshape(128, 124))
    vec = lambda v: f(inputs[v])[0].reshape(4, 128).T
    cvec = f(np.concatenate([vec("conv_b"), vec("conv_ln_g"), vec("conv_ln_b")], axis=1))
    post_g = f(inputs["post_ln_g"])[0][None, :]
    post_b = f(inputs["post_ln_b"])[0][None, :]
    cosT, sinT = _rope_tables(SEQ)
    cmat = _const_mats()
    maps = []
    for ci in range(ncores):
        b0 = ci * NB
        cc = np.concatenate([c[b0:b0 + NB], c_ctx[None, :]], axis=0)
        cT = f(cc.T.reshape(8, 128, NB + 1).transpose(1, 0, 2).reshape(128, 8 * (NB + 1)))
        maps.append({
            "x": f(x[b0:b0 + NB].reshape(NB * SEQ, D)),
            "ctx": f(ctx[b0:b0 + NB].reshape(NB * CTXL, D)),
            "cT": cT, "w_ada": w_ada, "b_adaT": b_adaT, "w_in": w_in_p, "w_out": w_out_p, "sinkp": sinkp,
            "conv_wT": conv_wT, "cvec": cvec, "post_g": post_g, "post_b": post_b, "cosT": cosT, "sinT": sinT,
            "cmat": cmat,
        })
    return maps


def kernel(**inputs):
    B, SEQ, _ = inputs["x"].shape
    NB = B // NCORES
    nc = build(NB, SEQ)
    maps = make_in_maps(inputs, NCORES, NB, SEQ)
    res = run_bass_kernel_spmd(nc, maps, core_ids=list(range(NCORES)))
    outs = [r["out"].reshape(NB, SEQ, D) for r in res.results]
    return np.concatenate(outs, axis=0).astype(np.float32)
```

```python
import contextlib
import numpy as np
import concourse.bass as bass
import concourse.mybir as mybir
from concourse.bass_utils import run_bass_kernel_spmd

F32 = mybir.dt.float32
BF16 = mybir.dt.bfloat16
AF = mybir.ActivationFunctionType
ALU = mybir.AluOpType

D = 1024
INW = 2816
CTXL = 256
NCORES = 8
ALPHA = 2.0 ** 0.25
LN_EPS = 1e-6
QO, KO, VO, AGO, CAO, CBO, CGO = 0, 512, 640, 768, 1280, 1792, 2304

COMPUTE = ("pe", "act", "dve", "pool")
NDMA = 8


class Sched:
    def __init__(self, dma_queues=("sp",)):
        self.ops = []
        self.dma_queues = tuple(dma_queues)

    def add(self, eng, fn, r=(), w=(), q=None):
        if not getattr(self, "enabled", True) and fn is not None:
            return
        r, w = list(r), list(w)
        for k in list(r):
            if isinstance(k, tuple) and k[0] == "ps":
                r.remove(k)
                if k not in w:
                    w.append(k)
        if q is None and eng in self.dma_queues:
            q = eng
        self.ops.append({"eng": eng, "fn": fn, "r": tuple(r), "w": tuple(w), "q": q})

    def analyse(self):
        ops = self.ops
        last_w, readers = {}, {}
        qcount = {q: 0 for q in self.dma_queues}
        qhist = {q: [] for q in self.dma_queues}
        for i, op in enumerate(ops):
            e = op["eng"]
            isdma = op["q"] is not None
            strong, weak = set(), set()
            for k in op["r"] + op["w"]:
                if k in last_w:
                    strong.add(last_w[k])
            for k in op["w"]:
                for d in readers.get(k, ()):
                    weak.add(d)
            deps = set()
            for d in strong | weak:
                if d == i:
                    continue
                de = ops[d]["eng"]
                if de == e and not isdma:
                    if e == "pe":
                        continue
                deps.add(d)
            if isdma:
                qk = op["q"]
                n = qcount[qk]
                op["dma_idx"] = n
                if n >= NDMA:
                    deps.add(qhist[qk][n - NDMA])
                qhist[qk].append(i)
                qcount[qk] = n + 1
            op["deps"] = deps
            for k in op["r"]:
                readers.setdefault(k, []).append(i)
            for k in op["w"]:
                last_w[k] = i
                readers[k] = []
        needed = set()
        for op in ops:
            needed |= op["deps"]
        cnt = {e: 0 for e in COMPUTE}
        for i, op in enumerate(ops):
            e = op["eng"]
            if op["q"] is None:
                if i in needed:
                    cnt[e] += 1
                    op["tok"] = (e, cnt[e])
                else:
                    op["tok"] = None
            else:
                n = op["dma_idx"]
                op["tok"] = ((op["q"], n % NDMA), 16 * (n // NDMA + 1))

    def emit_engine(self, e, eng, sems):
        ops = self.ops
        waited = {}
        for i, op in enumerate(ops):
            if op["eng"] != e:
                continue
            need = {}
            for d in op["deps"]:
                sk, val = ops[d]["tok"]
                if need.get(sk, 0) < val:
                    need[sk] = val
            for sk, val in need.items():
                if waited.get(sk, 0) >= val:
                    continue
                eng.wait_ge(sems[sk], val)
                waited[sk] = val
            if op["fn"] is None:
                continue
            ins = op["fn"](eng)
            tok = op["tok"]
            if tok is not None:
                ins.then_inc(sems[tok[0]], 1 if op["q"] is None else 16)

    def sem_keys(self):
        keys = list(COMPUTE)
        for q in self.dma_queues:
            keys += [(q, j) for j in range(NDMA)]
        return keys


def build(NB, SEQ):
    NT = SEQ // 128
    NS = SEQ // 512
    NB1 = NB + 1
    NG = NB * NS
    nc = bass.Bass("TRN2", target_bir_lowering=False)

    def dt(name, shape, kind="ExternalInput"):
        return nc.dram_tensor(name, shape, F32, kind=kind).ap()

    x_d = dt("x", [NB * SEQ, D])
    ctx_d = dt("ctx", [NB * CTXL, D])
    cT_d = dt("cT", [128, 8 * NB1])
    wada_d = dt("w_ada", [D, 3 * D])
    bada_d = dt("b_adaT", [128, 24])
    win_d = dt("w_in", [D, INW])
    wout_d = dt("w_out", [D, D])
    sink_d = dt("sinkp", [1, 8])
    cw_d = dt("conv_wT", [128, 124])
    cvec_d = dt("cvec", [128, 12])
    pg_d = dt("post_g", [1, D])
    pb_d = dt("post_b", [1, D])
    cos_d = dt("cosT", [128, SEQ])
    sin_d = dt("sinT", [128, SEQ])
    cm_d = dt("cmat", [128, 512])
    out_d = dt("out", [NB * SEQ, D], "ExternalOutput")
    diag_d = nc.dram_tensor("diag_scr", [128, 4, 32 * 128], BF16, kind="Internal").ap()

    S = Sched(dma_queues=("sp", "poolq"))
    st = contextlib.ExitStack()

    def sb(name, shape, dtype=F32):
        return st.enter_context(nc.sbuf_tensor(name, shape, dtype))

    w_in_bf = sb("w_in_bf", [128, 8, INW], BF16)
    w_out_bf = sb("w_out_bf", [128, 8, D], BF16)
    stage = sb("stage", [128, 5, 1024])
    identf = sb("identf", [128, 128])
    onesf = sb("onesf", [128, 128])
    cb16 = sb("cb16", [128, 4, 128], BF16)
    ident_bf, R_bf, maskA, maskB = (cb16[:, i, :] for i in range(4))
    ones512 = sb("ones512", [128, 128], BF16)
    cs = sb("cs", [128, 2, 2, 512])
    cT_sb = sb("cT_sb", [128, 8, NB1])
    scT = sb("scT", [128, 8, NB1])
    bada = sb("bada", [128, 24])
    modT = sb("modT", [128, 24, NB1])
    scale1T = sb("scale1T", [128, 8, NB1])
    gate_bc = sb("gate_bc", [128, D])
    g_bc = sb("g_bc", [128, D])
    b_bc = sb("b_bc", [128, D])
    hT = sb("hT", [128, 2, 8, 512], BF16)
    kT = sb("kT", [128, SEQ + CTXL], BF16)
    Vaug = sb("Vaug", [128, NT + 2, 192], BF16)
    uT = sb("uT", [128, 4, SEQ + 30], BF16)
    qT = sb("qT", [128, 4, 512], BF16)
    xn_bf = sb("xn_bf", [128, 2, D], BF16)
    NPT = 4
    PT = sb("PT", [128, NPT, 512], BF16)
    ropeb = sb("ropeb", [128, 512], BF16)
    mixA = sb("mixA", [128, 2, 4, 512], BF16)
    mixC = sb("mixC", [128, 4, 512], BF16)
    NTF = 4
    TF = sb("TF", [128, NTF, 512])
    sag2 = sb("sag2", [128, 4, 512], BF16)
    scg2 = sb("scg2", [128, 4, 512], BF16)
    convf = sb("convf", [128, 4, 512])
    conv_bf = sb("conv_bf", [128, 4, 512], BF16)
    NDG = 3
    diag = sb("diag", [128, NDG, 4, 128], BF16)
    numA = sb("numA", [128, 512])
    denA = sb("denA", [128, 512])
    cvec = sb("cvec_sb", [128, 12])
    stats = sb("stats", [128, 4, 2, 6])
    mv = sb("mv", [128, 4, 2])
    rs = sb("rs", [128, 4, 2])
    expo = sb("expo", [128, 4])
    st8 = sb("st8", [128, 20])
    sinkrow = sb("sinkrow", [1, 8])
    esr = sb("esr", [1, 8])
    esrow = sb("esrow", [1, 8], BF16)
    sinkL = sb("sinkL", [1, 2, 128], BF16)
    ps = [st.enter_context(nc.psum_tensor(f"ps{i}", [128, 512], F32)) for i in range(8)]
    PB = [0, 1, 2, 3]
    TM = 3
    SB_ = [4, 5]
    OB = [6, 7]
    pcnt = [0]

    def next_p():
        b = PB[pcnt[0] % len(PB)]
        pcnt[0] += 1
        return b

    tfc = [0]

    def next_tf():
        i = tfc[0] % NTF
        tfc[0] += 1
        return i

    cwh = TF[:, 3, 0:124]

    def xin(sl):
        return stage[:, sl, :]

    xres = stage[:, 2, :]

    def Rt(sl):
        return stage[:, 3 + sl, :]

    S.add("sp", lambda e: e.dma_start(out=identf[:], in_=cm_d[:, 0:128]), w=["identf"])
    S.add("sp", lambda e: e.dma_start(out=TF[:, 0, :], in_=cm_d), w=[("tf", 0)])
    S.add("dve", lambda e: e.tensor_copy(out=cb16[:].rearrange("p a b -> p (a b)"), in_=TF[:, 0, :]),
          r=[("tf", 0)], w=["cb16"])
    MASK_PE = True
    if MASK_PE:
      S.add("dve", lambda e: e.tensor_scalar(out=cb16[:, 2:4, :].rearrange("p a b -> p (a b)"),
                                           in0=cb16[:, 2:4, :].rearrange("p a b -> p (a b)"), scalar1=-1.0, scalar2=30000.0,
                                           op0=ALU.add, op1=ALU.mult), r=["cb16"], w=["cb16"])
    S.add("pool", lambda e: e.memset(ones512[:], 1.0 / 512.0), w=["ones512"])
    S.add("pool", lambda e: e.memset(expo[:], -0.5), w=["expo"])
    S.add("pool", lambda e: e.memset(onesf[:], 1.0), w=["onesf"])
    S.add("pool", lambda e: e.memset(Vaug[:].rearrange("p a b -> p (a b)"), 1.0), w=[("V", t) for t in range(NT + 2)])
    S.add("pool", lambda e: e.memset(uT[:].rearrange("p a b -> p (a b)"), 0.0), w=[("u", s) for s in range(-1, NS + 1)])
    S.add("pool", lambda e: e.memset(sinkL[:].rearrange("p a b -> p (a b)"), 0.0), w=["sinkL"])

    def _sl(e):
        e.memset(sinkL[:, 0, 64:128], 1.0)
        return e.memset(sinkL[:, 1, 0:64], 1.0)
    S.add("pool", _sl, w=["sinkL"])
    S.add("sp", lambda e: e.dma_start(out=bada[:], in_=bada_d), w=["bada"])
    S.add("sp", lambda e: e.dma_start(out=cvec[:], in_=cvec_d), w=["cvec"])
    S.add("sp", lambda e: e.dma_start(out=cwh, in_=cw_d), w=[("tf", 3)])
    S.add("sp", lambda e: e.dma_start(out=sinkrow[:], in_=sink_d), w=["sinkrow"])
    S.add("sp", lambda e: e.dma_start(out=g_bc[:], in_=pg_d.partition_broadcast(128)), w=["g_bc"])
    S.add("sp", lambda e: e.dma_start(out=b_bc[:], in_=pb_d.partition_broadcast(128)), w=["b_bc"])
    S.add("sp", lambda e: e.dma_start(out=cT_sb[:].rearrange("p a b -> p (a b)"), in_=cT_d), w=["cT"])
    S.add("dve", lambda e: e.tensor_scalar(out=cwh, in0=cwh, scalar1=0.5, scalar2=None, op0=ALU.mult),
          r=[("tf", 3)], w=[("tf", 3)])
    S.add("act", lambda e: e.activation(out=esr[:], in_=sinkrow[:], func=AF.Exp), r=["sinkrow"], w=["esr"])

    S.add("dve", lambda e: e.tensor_copy(out=esrow[:], in_=esr[:]), r=["esr"], w=["esrow"])
    cflat = cT_sb[:].rearrange("p a b -> p (a b)")
    sflat = scT[:].rearrange("p a b -> p (a b)")
    S.add("act", lambda e: e.activation(out=sflat, in_=cflat, func=AF.Tanh, scale=0.5), r=["cT"], w=["scT"])
    S.add("dve", lambda e: e.scalar_tensor_tensor(out=sflat, in0=sflat, scalar=1.0, in1=cflat, op0=ALU.add,
                                                  op1=ALU.mult), r=["scT", "cT"], w=["scT"])
    S.add("dve", lambda e: e.tensor_scalar(out=sflat, in0=sflat, scalar1=0.5, scalar2=None, op0=ALU.mult),
          r=["scT"], w=["scT"])

    stg_regions = [[("stg", 0), ("stg", 1)], [("stg", 2), ("stg", 3)]]

    def stg_slot(sl):
        return stage[:, 2 * sl:2 * sl + 2, :].rearrange("p a b -> p (a b)")

    wada_v = wada_d.rearrange("(kc p) n -> p kc n", p=128)
    wout_v = wout_d.rearrange("(kc p) n -> p kc n", p=128)
    modps = ps[TM][:, 0:24 * NB1].rearrange("p (a b) -> p a b", b=NB1)
    wslot = [TF[:, 0:3, :].rearrange("p a b -> p (a b)"), convf[:].rearrange("p a b -> p (a b)")]
    wslot_regions = [[("tf", 0), ("tf", 1), ("tf", 2)], [("convf", c) for c in range(4)]]
    HW = INW // 2
    wtasks = [("in", kc, part) for kc in range(8) for part in range(2)] + [("out", kc, 0) for kc in range(8)]
    wi = [0]

    def emit_wtask():
        if wi[0] >= len(wtasks):
            return
        idx = wi[0]
        wi[0] += 1
        kind, kc, part = wtasks[idx]
        sl = idx % 2
        regs = wslot_regions[sl]
        eng = "dve" if idx % 2 == 0 else "act"
        if kind == "in":
            c0 = part * HW
            sv = wslot[sl][:, 0:HW]
            src = win_d[kc * 128:(kc + 1) * 128, c0:c0 + HW]
            dst = w_in_bf[:, kc, c0:c0 + HW]
            wreg = [("w_in", kc, part)]
        else:
            sv = wslot[sl][:, 0:1024]
            src = wout_d[kc * 128:(kc + 1) * 128, :]
            dst = w_out_bf[:, kc, :]
            wreg = [("w_out", kc)]
        S.add("sp", lambda e: e.dma_start(out=sv, in_=src), w=regs)
        if eng == "act":
            S.add("act", lambda e: e.activation(out=dst, in_=sv, func=AF.Copy), r=regs, w=wreg)
        else:
            S.add("dve", lambda e: e.tensor_copy(out=dst, in_=sv), r=regs, w=wreg)

    for blk in range(12):
        sl = blk % 2
        sv = stg_slot(sl).rearrange("p (kc n) -> p kc n", n=256)

        def _ld(e, sv=sv, blk=blk):
            return e.dma_start(out=sv, in_=wada_v[:, :, blk * 256:(blk + 1) * 256])
        S.add("sp", _ld, w=stg_regions[sl])

        def _mm(e, sv=sv, blk=blk):
            for nn in range(2):
                n = blk * 2 + nn
                for kc in range(8):
                    ins = e.matmul(modps[:, n, :], lhsT=sv[:, kc, nn * 128:(nn + 1) * 128], rhs=scT[:, kc, :],
                                   start=(kc == 0), stop=(kc == 7))
            return ins
        S.add("pe", _mm, r=stg_regions[sl] + ["scT"], w=[("ps", TM)])
        emit_wtask()
        emit_wtask()
    while wi[0] < len(wtasks):
        emit_wtask()

    def _modev(e):
        for n in range(24):
            ins = e.activation(out=modT[:, n, :], in_=modps[:, n, :], func=AF.Identity, bias=bada[:, n:n + 1])
        return ins
    S.add("act", _modev, r=[("ps", TM), "bada"], w=["modT"])
    S.add("dve", lambda e: e.tensor_scalar(out=scale1T[:].rearrange("p a b -> p (a b)"),
                                           in0=modT[:, 8:16, :].rearrange("p a b -> p (a b)"), scalar1=1.0,
                                           scalar2=None, op0=ALU.add), r=["modT"], w=["scale1T"])
    W_IN_R = [("w_in", kc, p) for kc in range(8) for p in range(2)]
    W_OUT_R = [("w_out", kc) for kc in range(8)]
    stg_bf = stage[:, 0:4, :].rearrange("p a b -> p (a b)").bitcast(BF16)
    identb31 = identf[:].unsqueeze(1).to_broadcast([128, 31, 128])
    for c in range(4):
        sl = c % 2
        dv = stg_bf[:, sl * 4096:sl * 4096 + 31 * 128]
        eng = "dve" if c % 2 == 0 else "pool"
        S.add(eng, lambda e, dv=dv, c=c: e.tensor_tensor(
            out=dv.rearrange("p (a b) -> p a b", b=128), in0=identb31,
            in1=cwh[:, c * 31:(c + 1) * 31].unsqueeze(2).to_broadcast([128, 31, 128]), op=ALU.mult),
            r=["identf", ("tf", 3)], w=stg_regions[sl])
        S.add("sp", lambda e, dv=dv, c=c: e.dma_start(out=diag_d[:, c, 0:31 * 128], in_=dv), r=stg_regions[sl],
              w=[("diag_d", c)])

    lncnt = [0, 0]

    def layernorm_stats(src_ap, src_regions, width=1024, pool=0):
        sl = 2 * pool + lncnt[pool] % 2
        lncnt[pool] += 1
        nch = width // 512

        def _bn(e):
            for c in range(nch):
                ins = e.bn_stats(out=stats[:, sl, c, :], in_=src_ap[:, c * 512:(c + 1) * 512])
            return ins
        S.add("dve", _bn, r=src_regions, w=[("stats", sl)])
        S.add("dve", lambda e: e.bn_aggr(out=mv[:, sl, :], in_=stats[:, sl, 0:nch, :]), r=[("stats", sl)],
              w=[("mv", sl)])
        S.add("dve", lambda e: e.tensor_scalar(out=rs[:, sl, 0:1], in0=mv[:, sl, 1:2], scalar1=LN_EPS, scalar2=None,
                                               op0=ALU.add), r=[("mv", sl)], w=[("rs", sl)])
        S.add("pool", lambda e: e.tensor_tensor(out=rs[:, sl, 0:1], in0=rs[:, sl, 0:1], in1=expo[:, 0:1], op=ALU.pow),
              r=[("rs", sl), "expo"], w=[("rs", sl)])
        return sl

    xcnt = [0]

    def ln_part(src_dram_rows):
        xs = xcnt[0] % 2
        xcnt[0] += 1
        S.add("sp", lambda e: e.dma_start(out=xin(xs), in_=src_dram_rows), w=[("stg", xs)])
        sl = layernorm_stats(xin(xs), [("stg", xs)])
        S.add("pool", lambda e: e.tensor_scalar(out=rs[:, sl, 1:2], in0=mv[:, sl, 0:1], scalar1=-1.0,
                                                scalar2=rs[:, sl, 0:1], op0=ALU.mult, op1=ALU.mult),
              r=[("mv", sl), ("rs", sl)], w=[("rs2", sl)])
        S.add("pool", lambda e: e.tensor_scalar(out=xn_bf[:, xs, :], in0=xin(xs), scalar1=rs[:, sl, 0:1],
                                                scalar2=rs[:, sl, 1:2], op0=ALU.mult, op1=ALU.add),
              r=[("stg", xs), ("rs", sl), ("rs2", sl)], w=[("xn", xs)])
        return xs

    def tr_part(xs, bcol, dst_hT_ap, dst_regions):
        tb0 = next_p()
        tb1 = next_p()
        tp0 = ps[tb0][:].bitcast(BF16).rearrange("p (a b) -> p a b", b=128)
        tp1 = ps[tb1][:].bitcast(BF16).rearrange("p (a b) -> p a b", b=128)

        def _tr0(e):
            for kc in range(4):
                ins = e.transpose(out=tp0[:, kc, :], in_=xn_bf[:, xs, kc * 128:(kc + 1) * 128], identity=ident_bf)
            return ins
        S.add("pe", _tr0, r=[("xn", xs), "cb16"], w=[("ps", tb0)])

        def _tr1(e):
            for kc in range(4, 8):
                ins = e.transpose(out=tp1[:, kc - 4, :], in_=xn_bf[:, xs, kc * 128:(kc + 1) * 128], identity=ident_bf)
            return ins
        S.add("pe", _tr1, r=[("xn", xs), "cb16"], w=[("ps", tb1)])

        def _ev0(e):
            for kc in range(4):
                ins = e.activation(out=dst_hT_ap[:, kc, :], in_=tp0[:, kc, :], func=AF.Identity,
                                   scale=scale1T[:, kc, bcol:bcol + 1], bias=modT[:, kc, bcol:bcol + 1])
            return ins
        S.add("act", _ev0, r=[("ps", tb0), "scale1T", "modT"], w=[dst_regions[0]])

        def _ev1(e):
            for kc in range(4, 8):
                ins = e.tensor_scalar(out=dst_hT_ap[:, kc, :], in0=tp1[:, kc - 4, :], scalar1=scale1T[:, kc, bcol:bcol + 1],
                                      scalar2=modT[:, kc, bcol:bcol + 1], op0=ALU.mult, op1=ALU.add)
            return ins
        S.add("dve", _ev1, r=[("ps", tb1), "scale1T", "modT"], w=[dst_regions[1]])

    def HTR(hs):
        return [(("hT", hs), "lo"), (("hT", hs), "hi")]

    def proj_fm(col0, rhs_ap, rhs_regions, n):
        b = next_p()

        def _mm(e):
            for kc in range(8):
                ins = e.matmul(ps[b][:, 0:n], lhsT=w_in_bf[:, kc, col0:col0 + 128], rhs=rhs_ap[:, kc, :],
                               start=(kc == 0), stop=(kc == 7))
            return ins
        S.add("pe", _mm, r=W_IN_R + list(rhs_regions), w=[("ps", b)])
        return b

    def rope(b, csl, dst_ap, dst_regions):
        t1 = next_tf()
        t2 = next_tf()
        rb = next_p()
        S.add("act", lambda e: e.activation(out=ropeb[:], in_=ps[b][:], func=AF.Copy), r=[("ps", b)], w=["ropeb"])
        S.add("pe", lambda e: e.matmul(ps[rb][:], lhsT=R_bf, rhs=ropeb[:], start=True, stop=True),
              r=["ropeb", "cb16"], w=[("ps", rb)])
        S.add("dve", lambda e: e.tensor_tensor(out=TF[:, t1, :], in0=ps[b][:], in1=cs[:, csl, 0, :], op=ALU.mult),
              r=[("ps", b), ("cs", csl)], w=[("tf", t1)])
        S.add("dve", lambda e: e.tensor_tensor(out=TF[:, t2, :], in0=ps[rb][:], in1=cs[:, csl, 1, :], op=ALU.mult),
              r=[("ps", rb), ("cs", csl)], w=[("tf", t2)])
        S.add("pool", lambda e: e.tensor_tensor(out=dst_ap, in0=TF[:, t1, :], in1=TF[:, t2, :], op=ALU.add),
              r=[("tf", t1), ("tf", t2)], w=dst_regions)

    def gated_tanh(b, dst_ap, dst_regions):
        t = next_tf()
        S.add("act", lambda e: e.activation(out=TF[:, t, :], in_=ps[b][:], func=AF.Tanh, scale=0.5),
              r=[("ps", b)], w=[("tf", t)])
        S.add("dve", lambda e: e.scalar_tensor_tensor(out=dst_ap, in0=TF[:, t, :], scalar=1.0, in1=ps[b][:],
                                                      op0=ALU.add, op1=ALU.mult),
              r=[("tf", t), ("ps", b)], w=dst_regions)

    def stage_ctx(b, hs):
        hc = hT[:, hs, :, 0:CTXL]
        xs0 = ln_part(ctx_d[b * CTXL:b * CTXL + 128, :])
        xs1 = ln_part(ctx_d[b * CTXL + 128:b * CTXL + 256, :])
        tr_part(xs0, NB, hT[:, hs, :, 0:128], HTR(hs))
        tr_part(xs1, NB, hT[:, hs, :, 128:256], HTR(hs))
        pb = proj_fm(KO, hc, HTR(hs), CTXL)
        S.add("act", lambda e: e.activation(out=kT[:, SEQ:SEQ + CTXL], in_=ps[pb][:, 0:CTXL], func=AF.Copy),
              r=[("ps", pb)], w=[("k", NT), ("k", NT + 1)])
        vb = next_p()
        vps = ps[vb][:, 0:256].rearrange("p (a b) -> p a b", b=128)

        def _mm(e):
            for ct in range(2):
                for kc in range(8):
                    ins = e.matmul(vps[:, ct, :], lhsT=hT[:, hs, kc, ct * 128:(ct + 1) * 128],
                                   rhs=w_in_bf[:, kc, VO:VO + 128], start=(kc == 0), stop=(kc == 7))
            return ins
        S.add("pe", _mm, r=W_IN_R + HTR(hs), w=[("ps", vb)])

        def _ev(e):
            e.tensor_copy(out=Vaug[:, NT:NT + 2, 0:64], in_=vps[:, :, 0:64])
            return e.tensor_copy(out=Vaug[:, NT:NT + 2, 128:192], in_=vps[:, :, 64:128])
        S.add("dve", _ev, r=[("ps", vb)], w=[("V", NT), ("V", NT + 1)])

    def gate_broadcast(b):
        for half in range(2):
            gb = next_p()
            for c4 in range(4):
                kc = half * 4 + c4
                t = next_tf()
                S.add("dve", lambda e, kc=kc, t=t: e.tensor_copy(out=TF[:, t, 0:128],
                                                               in_=modT[:, 16 + kc, b:b + 1].to_broadcast([128, 128])),
                      r=["modT"], w=[("tf", t)])
                S.add("pe", lambda e, c4=c4, gb=gb, t=t: e.matmul(ps[gb][:, c4 * 128:(c4 + 1) * 128], lhsT=TF[:, t, 0:128],
                                                                 rhs=identf[:], start=True, stop=True),
                      r=[("tf", t), "identf"], w=[("ps", gb)])
            S.add("act", lambda e, gb=gb, half=half: e.activation(out=gate_bc[:, half * 512:(half + 1) * 512], in_=ps[gb][:],
                                                                func=AF.Copy), r=[("ps", gb)], w=["gate_bc"])

    prefetched = {}

    def stream_A(G):
        b, s = divmod(G, NS)
        hs = G % 2
        csl = G % 2
        S.add("sp", lambda e: e.dma_start(out=cs[:, csl, 0, :], in_=cos_d[:, s * 512:(s + 1) * 512]), w=[("cs", csl)])
        S.add("sp", lambda e: e.dma_start(out=cs[:, csl, 1, :], in_=sin_d[:, s * 512:(s + 1) * 512]), w=[("cs", csl)])
        r0 = b * SEQ + s * 512
        if G in prefetched:
            xsl = [prefetched.pop(G)]
        else:
            xsl = [ln_part(x_d[r0:r0 + 128, :])]
            yield
        for i in range(4):
            if i < 3:
                xsl.append(ln_part(x_d[r0 + (i + 1) * 128:r0 + (i + 2) * 128, :]))
            tr_part(xsl[i], b, hT[:, hs, :, i * 128:(i + 1) * 128], HTR(hs))
            yield
        hv = hT[:, hs, :, :]
        pb = proj_fm(KO, hv, HTR(hs), 512)
        rope(pb, csl, kT[:, s * 512:(s + 1) * 512], [("k", 4 * s + i) for i in range(4)])
        yield
        vb = next_p()
        vps = ps[vb][:].rearrange("p (a b) -> p a b", b=128)

        def _mm(e):
            for i in range(4):
                for kc in range(8):
                    ins = e.matmul(vps[:, i, :], lhsT=hT[:, hs, kc, i * 128:(i + 1) * 128],
                                   rhs=w_in_bf[:, kc, VO:VO + 128], start=(kc == 0), stop=(kc == 7))
            return ins
        S.add("pe", _mm, r=W_IN_R + HTR(hs), w=[("ps", vb)])

        def _ev(e):
            e.tensor_copy(out=Vaug[:, 4 * s:4 * s + 4, 0:64], in_=vps[:, :, 0:64])
            return e.tensor_copy(out=Vaug[:, 4 * s:4 * s + 4, 128:192], in_=vps[:, :, 64:128])
        S.add("dve", _ev, r=[("ps", vb)], w=[("V", 4 * s + i) for i in range(4)])
        yield
        for c in range(4):
            pcb = proj_fm(CBO + c * 128, hv, HTR(hs), 512)
            t = next_tf()
            S.add("act", lambda e, pcb=pcb, t=t: e.activation(out=TF[:, t, :], in_=ps[pcb][:], func=AF.Tanh, scale=0.5),
                  r=[("ps", pcb)], w=[("tf", t)])
            pca = proj_fm(CAO + c * 128, hv, HTR(hs), 512)
            S.add("dve", lambda e, pca=pca, t=t, c=c: e.scalar_tensor_tensor(
                out=uT[:, c, 15 + s * 512:15 + (s + 1) * 512], in0=TF[:, t, :], scalar=1.0, in1=ps[pca][:],
                op0=ALU.add, op1=ALU.mult), r=[("tf", t), ("ps", pca)], w=[("u", s)])
            yield
        if G + 1 < NG and not (G % NS == 0 and G > 0):
            b2, s2 = divmod(G + 1, NS)
            r2 = b2 * SEQ + s2 * 512
            prefetched[G + 1] = ln_part(x_d[r2:r2 + 128, :])
            yield

    ptc = [0]
    KHY = 1

    def stream_H(G):
        b, s = divmod(G, NS)
        hs = G % 2
        csl = G % 2
        ms = G % 2
        hv = hT[:, hs, :, :]
        for j in range(4):
            pb = proj_fm(QO + j * 128, hv, HTR(hs), 512)
            rope(pb, csl, qT[:, j, :], ["q"])
            yield
        for j in range(4):
            pb = proj_fm(AGO + j * 128, hv, HTR(hs), 512)
            gated_tanh(pb, sag2[:, j, :], [("sag2", j)])
            yield
        for i in range(4):
            t = 4 * s + i
            kts = [(t, None), (NT, None), (NT + 1, None)]
            if t > 0:
                kts.append((t - 1, maskA))
            if t < NT - 1:
                kts.append((t + 1, maskB))
                if i == 3:
                    yield "needA"
            if True:
                pairs = [(g, kt, m) for (kt, m) in kts for g in range(2)]
            else:
                pairs = [(g, kt, m) for g in range(2) for (kt, m) in kts]
            nk = len(kts)
            info = []

            def emit_S(n, i=i):
                g, kt, m = pairs[n]
                sbk = SB_[n % 2]
                def _smm(e, g=g, kt=kt, sbk=sbk, m=m, i=i):
                    o = ps[sbk][:].rearrange("p (a b) -> p a b", b=128)
                    ins = e.matmul(o, lhsT=kT[g * 64:(g + 1) * 64, kt * 128:(kt + 1) * 128],
                                   rhs=qT[g * 64:(g + 1) * 64, :, i * 128:(i + 1) * 128], start=True, stop=(m is None or not MASK_PE))
                    if m is not None and MASK_PE:
                        ins = e.matmul(o, lhsT=ident_bf, rhs=m.unsqueeze(1).to_broadcast([128, 4, 128]), start=False,
                                       stop=True)
                    return ins
                S.add("pe", _smm, r=[("k", kt), "q", "cb16"], w=[("ps", sbk)])
                p = ptc[0] % NPT
                ptc[0] += 1
                S.add("act", lambda e, sbk=sbk, p=p: e.activation(out=PT[:, p, :], in_=ps[sbk][:], func=AF.Exp,
                                                                  scale=0.125), r=[("ps", sbk)], w=[("pt", p)])
                if m is not None and not MASK_PE:
                    S.add("dve", lambda e, p=p, m=m: e.tensor_tensor(
                        out=PT[:, p, :].rearrange("p (a b) -> p a b", b=128),
                        in0=PT[:, p, :].rearrange("p (a b) -> p a b", b=128),
                        in1=m.unsqueeze(1).to_broadcast([128, 4, 128]), op=ALU.mult),
                        r=[("pt", p), "cb16"], w=[("pt", p)])
                info.append(p)

            def emit_PV(n):
                g, kt, m = pairs[n]
                p = info[n]
                first = (n < 2) if True else (n % nk == 0)
                S.add("pe", lambda e, g=g, kt=kt, p=p, first=first: e.matmul(
                    ps[OB[g]][:], lhsT=Vaug[:, kt, 64 * g:64 * g + 128], rhs=PT[:, p, :], start=first, stop=False),
                    r=[("V", kt), ("pt", p)], w=[("ps", OB[g])])
                if (n >= len(pairs) - 2) if True else (n % nk == nk - 1):
                    S.add("pe", lambda e, g=g: e.matmul(
                        ps[OB[g]][:].rearrange("p (a b) -> p a b", b=128), lhsT=sinkL[:, g, :],
                        rhs=esrow[:, 4 * g:4 * g + 4].unsqueeze(2).to_broadcast([1, 4, 128]), start=False, stop=True),
                        r=["sinkL", "esrow"], w=[("ps", OB[g])])

            emit_S(0)
            emit_S(1)
            for n in range(0, len(pairs), 2):
                if n + 2 < len(pairs):
                    emit_S(n + 2)
                    emit_S(n + 3)
                emit_PV(n)
                emit_PV(n + 1)
                if (n // 2) % KHY == KHY - 1:
                    yield
            S.add("act", lambda e: e.activation(out=denA[0:64, :], in_=ps[OB[0]][64:128, :], func=AF.Copy),
                  r=[("ps", OB[0])], w=["denA0"])
            S.add("act", lambda e: e.activation(out=denA[64:128, :], in_=ps[OB[1]][0:64, :], func=AF.Copy),
                  r=[("ps", OB[1])], w=["denA1"])
            S.add("act", lambda e: e.activation(out=numA[0:64, :], in_=ps[OB[0]][0:64, :], func=AF.Copy),
                  r=[("ps", OB[0])], w=["numA0"])
            S.add("act", lambda e: e.activation(out=numA[64:128, :], in_=ps[OB[1]][64:128, :], func=AF.Copy),
                  r=[("ps", OB[1])], w=["numA1"])
            S.add("dve", lambda e: e.reciprocal(out=denA[:], in_=denA[:]), r=["denA0", "denA1"], w=["denA0", "denA1"])
            ta = next_tf()
            S.add("dve", lambda e, ta=ta: e.scalar_tensor_tensor(out=TF[:, ta, :], in0=numA[:], scalar=0.5, in1=denA[:],
                                                                 op0=ALU.mult, op1=ALU.mult),
                  r=["numA0", "numA1", "denA0", "denA1"], w=[("tf", ta)])
            S.add("pool", lambda e, i=i, ta=ta: e.tensor_tensor(
                out=mixA[:, ms, :, i * 128:(i + 1) * 128], in0=TF[:, ta, :].rearrange("p (a b) -> p a b", b=128),
                in1=sag2[:, :, i * 128:(i + 1) * 128], op=ALU.mult),
                r=[("tf", ta)] + [("sag2", j) for j in range(4)], w=[("mixA", ms, i)])
            yield
        yield "needT"
        for c in range(4):
            pb = proj_fm(CGO + c * 128, hv, HTR(hs), 512)
            gated_tanh(pb, scg2[:, c, :], [("scg2", c)])
            yield

    dgc = [0]
    fcnt = [0]

    def stream_T(G):
        b, s = divmod(G, NS)
        ms = G % 2
        if s == 0:
            gate_broadcast(b)
            yield
        for c in range(4):
            cbk = next_p()
            for j0 in range(0, 31, 4):
                nj = min(4, 31 - j0)
                dg = dgc[0] % NDG
                dgc[0] += 1
                S.add("sp", lambda e, dg=dg, c=c, j0=j0, nj=nj: e.dma_start(
                    out=diag[:, dg, 0:nj, :].rearrange("p a b -> p (a b)"),
                    in_=diag_d[:, c, j0 * 128:(j0 + nj) * 128]), r=[("diag_d", c)], w=[("diag", dg)])

                def _cm(e, dg=dg, c=c, j0=j0, nj=nj, cbk=cbk):
                    for jj in range(nj):
                        j = j0 + jj
                        ins = e.matmul(ps[cbk][:], lhsT=diag[:, dg, jj, :], rhs=uT[:, c, s * 512 + j:s * 512 + j + 512],
                                       start=(j == 0), stop=(j == 30))
                    return ins
                S.add("pe", _cm, r=[("diag", dg), ("u", s - 1), ("u", s), ("u", s + 1)], w=[("ps", cbk)])
            S.add("act", lambda e, c=c, cbk=cbk: e.activation(out=convf[:, c, :], in_=ps[cbk][:], func=AF.Identity,
                                                              bias=cvec[:, c:c + 1]), r=[("ps", cbk), "cvec"],
                  w=[("convf", c)])
            S.add("act", lambda e, c=c, cbk=cbk: e.activation(out=mixC[:, c, :], in_=ps[cbk][:], func=AF.Square,
                                                              bias=cvec[:, c:c + 1]), r=[("ps", cbk), "cvec"],
                  w=[("mixC", c)])
            S.add("pool", lambda e, c=c: e.tensor_copy(out=conv_bf[:, c, :], in_=convf[:, c, :]), r=[("convf", c)],
                  w=[("cbf", c)])
            yield
        sbk = next_p()

        def _st(e):
            for a in range(4):
                for c in range(4):
                    e.matmul(ps[sbk][:, a:a + 1], lhsT=conv_bf[:, c, a * 128:(a + 1) * 128], rhs=ones512[:, 0:1],
                             start=(c == 0), stop=(c == 3))
            for a in range(4):
                for c in range(4):
                    ins = e.matmul(ps[sbk][:, 4 + a:5 + a], lhsT=mixC[:, c, a * 128:(a + 1) * 128],
                                   rhs=ones512[:, 0:1], start=(c == 0), stop=(c == 3))
            return ins
        S.add("pe", _st, r=["ones512"] + [("cbf", c) for c in range(4)] + [("mixC", c) for c in range(4)],
              w=[("ps", sbk)])
        S.add("dve", lambda e: e.tensor_copy(out=st8[:, 0:8], in_=ps[sbk][:, 0:8]), r=[("ps", sbk)], w=["st8a"])
        S.add("dve", lambda e: e.tensor_tensor(out=st8[:, 8:12], in0=st8[:, 0:4], in1=st8[:, 0:4], op=ALU.mult),
              r=["st8a"], w=["st8b"])
        S.add("dve", lambda e: e.scalar_tensor_tensor(out=st8[:, 12:16], in0=st8[:, 4:8], scalar=LN_EPS, in1=st8[:, 8:12],
                                                      op0=ALU.add, op1=ALU.subtract), r=["st8a", "st8b"], w=["st8c"])
        S.add("pool", lambda e: e.tensor_tensor(out=st8[:, 12:16], in0=st8[:, 12:16], in1=expo[:, 0:4], op=ALU.pow),
              r=["st8c", "expo"], w=["st8c"])
        S.add("dve", lambda e: e.scalar_tensor_tensor(out=st8[:, 16:20], in0=st8[:, 0:4], scalar=-1.0, in1=st8[:, 12:16],
                                                      op0=ALU.mult, op1=ALU.mult), r=["st8a", "st8c"], w=["st8d"])
        tm = next_tf()
        tv = next_tf()
        identb = identf[:].unsqueeze(1).to_broadcast([128, 4, 128])
        S.add("dve", lambda e: e.tensor_tensor(out=TF[:, tm, :].rearrange("p (a b) -> p a b", b=128), in0=identb,
                                               in1=st8[:, 12:16].unsqueeze(2).to_broadcast([128, 4, 128]), op=ALU.mult),
              r=["identf", "st8c"], w=[("tf", tm)])
        S.add("dve", lambda e: e.tensor_tensor(out=TF[:, tv, :].rearrange("p (a b) -> p a b", b=128), in0=identb,
                                               in1=st8[:, 16:20].unsqueeze(2).to_broadcast([128, 4, 128]), op=ALU.mult),
              r=["identf", "st8d"], w=[("tf", tv)])
        rbk = next_p()
        nbk = next_p()
        S.add("pe", lambda e: e.matmul(ps[rbk][:], lhsT=onesf[:], rhs=TF[:, tm, :], start=True, stop=True),
              r=["onesf", ("tf", tm)], w=[("ps", rbk)])
        S.add("pe", lambda e: e.matmul(ps[nbk][:], lhsT=onesf[:], rhs=TF[:, tv, :], start=True, stop=True),
              r=["onesf", ("tf", tv)], w=[("ps", nbk)])
        def emit_z(c):
            S.add("dve", lambda e: e.tensor_scalar(out=convf[:, c, :], in0=convf[:, c, :], scalar1=cvec[:, 4 + c:5 + c],
                                                   scalar2=cvec[:, 8 + c:9 + c], op0=ALU.mult, op1=ALU.add),
                  r=[("convf", c), "cvec"], w=[("convf", c)])

        for c in range(4):
            S.add("dve", lambda e, c=c: e.tensor_tensor(out=convf[:, c, :], in0=convf[:, c, :], in1=ps[rbk][:],
                                                        op=ALU.mult), r=[("convf", c), ("ps", rbk)],
                  w=[("convf", c)])
            S.add("dve", lambda e, c=c: e.tensor_tensor(out=convf[:, c, :], in0=convf[:, c, :], in1=ps[nbk][:],
                                                        op=ALU.add), r=[("convf", c), ("ps", nbk)], w=[("convf", c)])
        emit_z(0)
        emit_z(1)
        yield
        for c in range(4):
            tz = next_tf()
            S.add("act", lambda e, c=c, tz=tz: e.activation(out=TF[:, tz, :], in_=convf[:, c, :], func=AF.Tanh, scale=0.5),
                  r=[("convf", c)], w=[("tf", tz)])
            if c + 2 < 4:
                emit_z(c + 2)
            S.add("dve", lambda e, c=c, tz=tz: e.scalar_tensor_tensor(out=TF[:, tz, :], in0=TF[:, tz, :], scalar=1.0,
                                                                      in1=convf[:, c, :], op0=ALU.add, op1=ALU.mult),
                  r=[("tf", tz), ("convf", c)], w=[("tf", tz)])
            S.add("dve", lambda e, c=c, tz=tz: e.scalar_tensor_tensor(out=mixC[:, c, :], in0=TF[:, tz, :], scalar=0.25,
                                                                      in1=scg2[:, c, :], op0=ALU.mult, op1=ALU.mult),
                  r=[("tf", tz), ("scg2", c)], w=[("mixC", c)])
            yield
        for i in range(4):
            fs = fcnt[0] % 2
            fcnt[0] += 1
            r0 = b * SEQ + s * 512 + i * 128
            S.add("sp", lambda e, r0=r0: e.dma_start(out=xres, in_=x_d[r0:r0 + 128, :]), w=[("stg", 2)])
            for half in range(2):
                yb = next_p()

                def _mm(e, yb=yb, half=half, i=i):
                    for kc in range(4):
                        e.matmul(ps[yb][:], lhsT=mixA[:, ms, kc, i * 128:(i + 1) * 128],
                                 rhs=w_out_bf[:, kc, half * 512:(half + 1) * 512], start=(kc == 0), stop=False)
                    for kc in range(4):
                        ins = e.matmul(ps[yb][:], lhsT=mixC[:, kc, i * 128:(i + 1) * 128],
                                       rhs=w_out_bf[:, 4 + kc, half * 512:(half + 1) * 512], start=False, stop=(kc == 3))
                    return ins
                S.add("pe", _mm, r=W_OUT_R + [("mixA", ms, i)] + [("mixC", c) for c in range(4)], w=[("ps", yb)])
                S.add("dve", lambda e, yb=yb, half=half, fs=fs: e.tensor_tensor(
                    out=Rt(fs)[:, half * 512:(half + 1) * 512], in0=ps[yb][:], in1=gate_bc[:, half * 512:(half + 1) * 512],
                    op=ALU.mult), r=[("ps", yb), "gate_bc"], w=[("stg", 3 + fs)])
            yield
            S.add("dve", lambda e, fs=fs: e.scalar_tensor_tensor(out=Rt(fs), in0=xres, scalar=ALPHA, in1=Rt(fs),
                                                                 op0=ALU.mult, op1=ALU.add),
                  r=[("stg", 2), ("stg", 3 + fs)], w=[("stg", 3 + fs)])
            sl = layernorm_stats(Rt(fs), [("stg", 3 + fs)], pool=1)
            yield
            S.add("dve", lambda e, sl=sl, fs=fs: e.tensor_scalar(out=Rt(fs), in0=Rt(fs), scalar1=mv[:, sl, 0:1],
                                                                 scalar2=rs[:, sl, 0:1], op0=ALU.subtract, op1=ALU.mult),
                  r=[("stg", 3 + fs), ("mv", sl), ("rs", sl)], w=[("stg", 3 + fs)])
            S.add("pool", lambda e, fs=fs: e.tensor_tensor(out=Rt(fs), in0=Rt(fs), in1=g_bc[:], op=ALU.mult),
                  r=[("stg", 3 + fs), "g_bc"], w=[("stg", 3 + fs)])
            S.add("pool", lambda e, fs=fs: e.tensor_tensor(out=Rt(fs), in0=Rt(fs), in1=b_bc[:], op=ALU.add),
                  r=[("stg", 3 + fs), "b_bc"], w=[("stg", 3 + fs)])
            S.add("pool", lambda e, fs=fs, r0=r0: e.dma_start(out=out_d[r0:r0 + 128, :], in_=Rt(fs)),
                  r=[("stg", 3 + fs)], w=[("out", r0)], q="poolq")
            yield

    def drain(g):
        if g is None:
            return
        for _ in g:
            pass

    stage_ctx(0, 1)
    for G in range(NG + 2):
        gA = stream_A(G) if G < NG else None
        gH = stream_H(G - 1) if 1 <= G <= NG else None
        gT = stream_T(G - 2) if 2 <= G <= NG + 1 else None
        live = {"A": gA, "H": gH, "T": gT}
        order = ("H", "A", "T")
        while any(v is not None for v in live.values()):
            for name in order:
                g = live[name]
                if g is None:
                    continue
                try:
                    v = next(g)
                except StopIteration:
                    live[name] = None
                    continue
                if v == "needA":
                    drain(live["A"])
                    live["A"] = None
                elif v == "needT":
                    drain(live["T"])
                    live["T"] = None
        if G < NG and G > 0 and G % NS == 0:
            stage_ctx(G // NS, (G + 1) % 2)

    out_regions = [("out", b * SEQ + t * 128) for b in range(NB) for t in range(NT)]
    S.add("sp", None, r=out_regions)

    S.analyse()
    sems = {}
    for k in S.sem_keys():
        nm = "s_" + (k if isinstance(k, str) else f"{k[0]}{k[1]}")
        sems[k] = st.enter_context(nc.semaphore(nm))
    with nc.Block() as block:
        @block.tensor
        def _(e):
            S.emit_engine("pe", e, sems)

        @block.scalar
        def _(e):
            S.emit_engine("act", e, sems)

        @block.vector
        def _(e):
            S.emit_engine("dve", e, sems)

        @block.gpsimd
        def _(e):
            S.emit_engine("pool", e, sems)

        @block.sync
        def _(e):
            S.emit_engine("sp", e, sems)
    st.close()
    return nc


def _head_perm():
    idx = []
    for j in range(4):
        idx += list(range(j * 64, (j + 1) * 64)) + list(range((4 + j) * 64, (5 + j) * 64))
    return np.array(idx)


def _rope_tables(seq):
    grid_w = 64
    rows = seq // grid_w
    row = np.repeat(np.arange(rows, dtype=np.float32), grid_w)
    col = np.tile(np.arange(grid_w, dtype=np.float32), rows)
    inv_freq = (np.float32(10000.0) ** (-np.arange(0, 32, 2, dtype=np.float32) / np.float32(32))).astype(np.float32)
    cosT = np.zeros((128, seq), np.float32)
    sinT = np.zeros((128, seq), np.float32)
    for p in range(128):
        d = p % 64
        pos = row if d < 32 else col
        i = d % 32
        ang = (pos * inv_freq[i % 16]).astype(np.float32)
        cosT[p] = np.cos(ang)
        sinT[p] = np.sin(ang) * (-1.0 if i < 16 else 1.0)
    return cosT, sinT


def _const_mats():
    ident = np.eye(128, dtype=np.float32)
    R = np.zeros((128, 128), np.float32)
    for m in range(128):
        partner = m + 16 if (m % 32) < 16 else m - 16
        R[partner, m] = 1.0
    k = np.arange(128)[:, None]
    q = np.arange(128)[None, :]
    maskA = (k >= q).astype(np.float32)
    maskB = (k <= q).astype(np.float32)
    return np.concatenate([ident, R, maskA, maskB], axis=1)


def make_in_maps(inputs, ncores, NB, SEQ):
    f = lambda a: np.ascontiguousarray(np.asarray(a, dtype=np.float32))
    x = f(inputs["x"])
    c = f(inputs["c"])
    ctx = f(inputs["ctx"])
    c_ctx = f(inputs["c_ctx"])
    w_ada = f(inputs["w_ada"])[0]
    b_ada = f(inputs["b_ada"])[0]
    w_in = f(inputs["w_in"])[0]
    w_out = f(inputs["w_out"])[0]
    sink = f(inputs["attn_sink"])[0]
    conv_w = f(inputs["conv_w"])[0]
    hp = _head_perm()
    cols = np.concatenate([hp, 512 + np.arange(256), 768 + hp, 1280 + np.arange(1536)])
    w_in_p = f(w_in[:, cols])
    rows = np.concatenate([hp, 512 + np.arange(512)])
    w_out_p = f(w_out[rows, :])
    b_adaT = f(b_ada.reshape(24, 128).T)
    sinkp = f(np.concatenate([sink[0:4], sink[4:8]])[None, :])
    conv_wT = f(conv_w.T.reshape(4, 128, 31).transpose(1, 0, 2).reshape(128, 124))
    vec = lambda v: f(inputs[v])[0].reshape(4, 128).T
    cvec = f(np.concatenate([vec("conv_b"), vec("conv_ln_g"), vec("conv_ln_b")], axis=1))
    post_g = f(inputs["post_ln_g"])[0][None, :]
    post_b = f(inputs["post_ln_b"])[0][None, :]
    cosT, sinT = _rope_tables(SEQ)
    cmat = _const_mats()
    maps = []
    for ci in range(ncores):
        b0 = ci * NB
        cc = np.concatenate([c[b0:b0 + NB], c_ctx[None, :]], axis=0)
        cT = f(cc.T.reshape(8, 128, NB + 1).transpose(1, 0, 2).reshape(128, 8 * (NB + 1)))
        maps.append({
            "x": f(x[b0:b0 + NB].reshape(NB * SEQ, D)),
            "ctx": f(ctx[b0:b0 + NB].reshape(NB * CTXL, D)),
            "cT": cT, "w_ada": w_ada, "b_adaT": b_adaT, "w_in": w_in_p, "w_out": w_out_p, "sinkp": sinkp,
            "conv_wT": conv_wT, "cvec": cvec, "post_g": post_g, "post_b": post_b, "cosT": cosT, "sinT": sinT,
            "cmat": cmat,
        })
    return maps


def kernel(**inputs):
    B, SEQ, _ = inputs["x"].shape
    NB = B // NCORES
    nc = build(NB, SEQ)
    maps = make_in_maps(inputs, NCORES, NB, SEQ)
    res = run_bass_kernel_spmd(nc, maps, core_ids=list(range(NCORES)))
    outs = [r["out"].reshape(NB, SEQ, D) for r in res.results]
    return np.concatenate(outs, axis=0).astype(np.float32)
```

```python
import contextlib
import numpy as np
import concourse.bass as bass
import concourse.mybir as mybir
from concourse.bass_utils import run_bass_kernel_spmd

F32 = mybir.dt.float32
BF16 = mybir.dt.bfloat16
AF = mybir.ActivationFunctionType
ALU = mybir.AluOpType

D = 1024
INW = 2816
CTXL = 256
NCORES = 8
ALPHA = 2.0 ** 0.25
LN_EPS = 1e-6
QO, KO, VO, AGO, CAO, CBO, CGO = 0, 512, 640, 768, 1280, 1792, 2304

COMPUTE = ("pe", "act", "dve", "pool")
NDMA = 8


class Sched:
    def __init__(self, dma_queues=("sp",)):
        self.ops = []
        self.dma_queues = tuple(dma_queues)

    def add(self, eng, fn, r=(), w=(), q=None):
        if not getattr(self, "enabled", True) and fn is not None:
            return
        r, w = list(r), list(w)
        for k in list(r):
            if isinstance(k, tuple) and k[0] == "ps":
                r.remove(k)
                if k not in w:
                    w.append(k)
        if q is None and eng in self.dma_queues:
            q = eng
        self.ops.append({"eng": eng, "fn": fn, "r": tuple(r), "w": tuple(w), "q": q})

    def analyse(self):
        ops = self.ops
        last_w, readers = {}, {}
        qcount = {q: 0 for q in self.dma_queues}
        qhist = {q: [] for q in self.dma_queues}
        for i, op in enumerate(ops):
            e = op["eng"]
            isdma = op["q"] is not None
            strong, weak = set(), set()
            for k in op["r"] + op["w"]:
                if k in last_w:
                    strong.add(last_w[k])
            for k in op["w"]:
                for d in readers.get(k, ()):
                    weak.add(d)
            deps = set()
            for d in strong | weak:
                if d == i:
                    continue
                de = ops[d]["eng"]
                if de == e and not isdma:
                    if e == "pe":
                        continue
                deps.add(d)
            if isdma:
                qk = op["q"]
                n = qcount[qk]
                op["dma_idx"] = n
                if n >= NDMA:
                    deps.add(qhist[qk][n - NDMA])
                qhist[qk].append(i)
                qcount[qk] = n + 1
            op["deps"] = deps
            for k in op["r"]:
                readers.setdefault(k, []).append(i)
            for k in op["w"]:
                last_w[k] = i
                readers[k] = []
        needed = set()
        for op in ops:
            needed |= op["deps"]
        cnt = {e: 0 for e in COMPUTE}
        for i, op in enumerate(ops):
            e = op["eng"]
            if op["q"] is None:
                if i in needed:
                    cnt[e] += 1
                    op["tok"] = (e, cnt[e])
                else:
                    op["tok"] = None
            else:
                n = op["dma_idx"]
                op["tok"] = ((op["q"], n % NDMA), 16 * (n // NDMA + 1))

    def emit_engine(self, e, eng, sems):
        ops = self.ops
        waited = {}
        for i, op in enumerate(ops):
            if op["eng"] != e:
                continue
            need = {}
            for d in op["deps"]:
                sk, val = ops[d]["tok"]
                if need.get(sk, 0) < val:
                    need[sk] = val
            for sk, val in need.items():
                if waited.get(sk, 0) >= val:
                    continue
                eng.wait_ge(sems[sk], val)
                waited[sk] = val
            if op["fn"] is None:
                continue
            ins = op["fn"](eng)
            tok = op["tok"]
            if tok is not None:
                ins.then_inc(sems[tok[0]], 1 if op["q"] is None else 16)

    def sem_keys(self):
        keys = list(COMPUTE)
        for q in self.dma_queues:
            keys += [(q, j) for j in range(NDMA)]
        return keys


def build(NB, SEQ):
    NT = SEQ // 128
    NS = SEQ // 512
    NB1 = NB + 1
    NG = NB * NS
    nc = bass.Bass("TRN2", target_bir_lowering=False)

    def dt(name, shape, kind="ExternalInput"):
        return nc.dram_tensor(name, shape, F32, kind=kind).ap()

    x_d = dt("x", [NB * SEQ, D])
    ctx_d = dt("ctx", [NB * CTXL, D])
    cT_d = dt("cT", [128, 8 * NB1])
    wada_d = dt("w_ada", [D, 3 * D])
    bada_d = dt("b_adaT", [128, 24])
    win_d = dt("w_in", [D, INW])
    wout_d = dt("w_out", [D, D])
    sink_d = dt("sinkp", [1, 8])
    cw_d = dt("conv_wT", [128, 124])
    cvec_d = dt("cvec", [128, 12])
    pg_d = dt("post_g", [1, D])
    pb_d = dt("post_b", [1, D])
    cos_d = dt("cosT", [128, SEQ])
    sin_d = dt("sinT", [128, SEQ])
    cm_d = dt("cmat", [128, 512])
    out_d = dt("out", [NB * SEQ, D], "ExternalOutput")
    diag_d = nc.dram_tensor("diag_scr", [128, 4, 32 * 128], BF16, kind="Internal").ap()

    S = Sched(dma_queues=("sp", "poolq"))
    st = contextlib.ExitStack()

    def sb(name, shape, dtype=F32):
        return st.enter_context(nc.sbuf_tensor(name, shape, dtype))

    w_in_bf = sb("w_in_bf", [128, 8, INW], BF16)
    w_out_bf = sb("w_out_bf", [128, 8, D], BF16)
    stage = sb("stage", [128, 5, 1024])
    identf = sb("identf", [128, 128])
    onesf = sb("onesf", [128, 128])
    cb16 = sb("cb16", [128, 4, 128], BF16)
    ident_bf, R_bf, maskA, maskB = (cb16[:, i, :] for i in range(4))
    ones512 = sb("ones512", [128, 128], BF16)
    cs = sb("cs", [128, 2, 2, 512])
    cT_sb = sb("cT_sb", [128, 8, NB1])
    scT = sb("scT", [128, 8, NB1])
    bada = sb("bada", [128, 24])
    modT = sb("modT", [128, 24, NB1])
    scale1T = sb("scale1T", [128, 8, NB1])
    gate_bc = sb("gate_bc", [128, D])
    g_bc = sb("g_bc", [128, D])
    b_bc = sb("b_bc", [128, D])
    hT = sb("hT", [128, 2, 8, 512], BF16)
    kT = sb("kT", [128, SEQ + CTXL], BF16)
    Vaug = sb("Vaug", [128, NT + 2, 192], BF16)
    uT = sb("uT", [128, 4, SEQ + 30], BF16)
    qT = sb("qT", [128, 4, 512], BF16)
    xn_bf = sb("xn_bf", [128, 2, D], BF16)
    NPT = 4
    PT = sb("PT", [128, NPT, 512], BF16)
    ropeb = sb("ropeb", [128, 512], BF16)
    mixA = sb("mixA", [128, 2, 4, 512], BF16)
    mixC = sb("mixC", [128, 4, 512], BF16)
    NTF = 4
    TF = sb("TF", [128, NTF, 512])
    sag2 = sb("sag2", [128, 4, 512], BF16)
    scg2 = sb("scg2", [128, 4, 512], BF16)
    convf = sb("convf", [128, 4, 512])
    conv_bf = sb("conv_bf", [128, 4, 512], BF16)
    NDG = 3
    diag = sb("diag", [128, NDG, 4, 128], BF16)
    numA = sb("numA", [128, 512])
    denA = sb("denA", [128, 512])
    cvec = sb("cvec_sb", [128, 12])
    stats = sb("stats", [128, 4, 2, 6])
    mv = sb("mv", [128, 4, 2])
    rs = sb("rs", [128, 4, 2])
    expo = sb("expo", [128, 4])
    st8 = sb("st8", [128, 20])
    sinkrow = sb("sinkrow", [1, 8])
    esr = sb("esr", [1, 8])
    esrow = sb("esrow", [1, 8], BF16)
    sinkL = sb("sinkL", [1, 2, 128], BF16)
    ps = [st.enter_context(nc.psum_tensor(f"ps{i}", [128, 512], F32)) for i in range(8)]
    PB = [0, 1, 2, 3]
    TM = 3
    SB_ = [4, 5]
    OB = [6, 7]
    pcnt = [0]

    def next_p():
        b = PB[pcnt[0] % len(PB)]
        pcnt[0] += 1
        return b

    tfc = [0]

    def next_tf():
        i = tfc[0] % NTF
        tfc[0] += 1
        return i

    cwh = TF[:, 3, 0:124]

    def xin(sl):
        return stage[:, sl, :]

    xres = stage[:, 2, :]

    def Rt(sl):
        return stage[:, 3 + sl, :]

    S.add("sp", lambda e: e.dma_start(out=identf[:], in_=cm_d[:, 0:128]), w=["identf"])
    S.add("sp", lambda e: e.dma_start(out=TF[:, 0, :], in_=cm_d), w=[("tf", 0)])
    S.add("dve", lambda e: e.tensor_copy(out=cb16[:].rearrange("p a b -> p (a b)"), in_=TF[:, 0, :]),
          r=[("tf", 0)], w=["cb16"])
    MASK_PE = True
    if MASK_PE:
      S.add("dve", lambda e: e.tensor_scalar(out=cb16[:, 2:4, :].rearrange("p a b -> p (a b)"),
                                           in0=cb16[:, 2:4, :].rearrange("p a b -> p (a b)"), scalar1=-1.0, scalar2=30000.0,
                                           op0=ALU.add, op1=ALU.mult), r=["cb16"], w=["cb16"])
    S.add("pool", lambda e: e.memset(ones512[:], 1.0 / 512.0), w=["ones512"])
    S.add("pool", lambda e: e.memset(expo[:], -0.5), w=["expo"])
    S.add("pool", lambda e: e.memset(onesf[:], 1.0), w=["onesf"])
    S.add("pool", lambda e: e.memset(Vaug[:].rearrange("p a b -> p (a b)"), 1.0), w=[("V", t) for t in range(NT + 2)])
    S.add("pool", lambda e: e.memset(uT[:].rearrange("p a b -> p (a b)"), 0.0), w=[("u", s) for s in range(-1, NS + 1)])
    S.add("pool", lambda e: e.memset(sinkL[:].rearrange("p a b -> p (a b)"), 0.0), w=["sinkL"])

    def _sl(e):
        e.memset(sinkL[:, 0, 64:128], 1.0)
        return e.memset(sinkL[:, 1, 0:64], 1.0)
    S.add("pool", _sl, w=["sinkL"])
    S.add("sp", lambda e: e.dma_start(out=bada[:], in_=bada_d), w=["bada"])
    S.add("sp", lambda e: e.dma_start(out=cvec[:], in_=cvec_d), w=["cvec"])
    S.add("sp", lambda e: e.dma_start(out=cwh, in_=cw_d), w=[("tf", 3)])
    S.add("sp", lambda e: e.dma_start(out=sinkrow[:], in_=sink_d), w=["sinkrow"])
    S.add("sp", lambda e: e.dma_start(out=g_bc[:], in_=pg_d.partition_broadcast(128)), w=["g_bc"])
    S.add("sp", lambda e: e.dma_start(out=b_bc[:], in_=pb_d.partition_broadcast(128)), w=["b_bc"])
    S.add("sp", lambda e: e.dma_start(out=cT_sb[:].rearrange("p a b -> p (a b)"), in_=cT_d), w=["cT"])
    S.add("dve", lambda e: e.tensor_scalar(out=cwh, in0=cwh, scalar1=0.5, scalar2=None, op0=ALU.mult),
          r=[("tf", 3)], w=[("tf", 3)])
    S.add("act", lambda e: e.activation(out=esr[:], in_=sinkrow[:], func=AF.Exp), r=["sinkrow"], w=["esr"])

    S.add("dve", lambda e: e.tensor_copy(out=esrow[:], in_=esr[:]), r=["esr"], w=["esrow"])
    cflat = cT_sb[:].rearrange("p a b -> p (a b)")
    sflat = scT[:].rearrange("p a b -> p (a b)")
    S.add("act", lambda e: e.activation(out=sflat, in_=cflat, func=AF.Tanh, scale=0.5), r=["cT"], w=["scT"])
    S.add("dve", lambda e: e.scalar_tensor_tensor(out=sflat, in0=sflat, scalar=1.0, in1=cflat, op0=ALU.add,
                                                  op1=ALU.mult), r=["scT", "cT"], w=["scT"])
    S.add("dve", lambda e: e.tensor_scalar(out=sflat, in0=sflat, scalar1=0.5, scalar2=None, op0=ALU.mult),
          r=["scT"], w=["scT"])

    stg_regions = [[("stg", 0), ("stg", 1)], [("stg", 2), ("stg", 3)]]

    def stg_slot(sl):
        return stage[:, 2 * sl:2 * sl + 2, :].rearrange("p a b -> p (a b)")

    wada_v = wada_d.rearrange("(kc p) n -> p kc n", p=128)
    wout_v = wout_d.rearrange("(kc p) n -> p kc n", p=128)
    modps = ps[TM][:, 0:24 * NB1].rearrange("p (a b) -> p a b", b=NB1)
    wslot = [TF[:, 0:3, :].rearrange("p a b -> p (a b)"), convf[:].rearrange("p a b -> p (a b)")]
    wslot_regions = [[("tf", 0), ("tf", 1), ("tf", 2)], [("convf", c) for c in range(4)]]
    HW = INW // 2
    wtasks = [("in", kc, part) for kc in range(8) for part in range(2)] + [("out", kc, 0) for kc in range(8)]
    wi = [0]

    def emit_wtask():
        if wi[0] >= len(wtasks):
            return
        idx = wi[0]
        wi[0] += 1
        kind, kc, part = wtasks[idx]
        sl = idx % 2
        regs = wslot_regions[sl]
        eng = "dve" if idx % 2 == 0 else "act"
        if kind == "in":
            c0 = part * HW
            sv = wslot[sl][:, 0:HW]
            src = win_d[kc * 128:(kc + 1) * 128, c0:c0 + HW]
            dst = w_in_bf[:, kc, c0:c0 + HW]
            wreg = [("w_in", kc, part)]
        else:
            sv = wslot[sl][:, 0:1024]
            src = wout_d[kc * 128:(kc + 1) * 128, :]
            dst = w_out_bf[:, kc, :]
            wreg = [("w_out", kc)]
        S.add("sp", lambda e: e.dma_start(out=sv, in_=src), w=regs)
        if eng == "act":
            S.add("act", lambda e: e.activation(out=dst, in_=sv, func=AF.Copy), r=regs, w=wreg)
        else:
            S.add("dve", lambda e: e.tensor_copy(out=dst, in_=sv), r=regs, w=wreg)

    for blk in range(12):
        sl = blk % 2
        sv = stg_slot(sl).rearrange("p (kc n) -> p kc n", n=256)

        def _ld(e, sv=sv, blk=blk):
            return e.dma_start(out=sv, in_=wada_v[:, :, blk * 256:(blk + 1) * 256])
        S.add("sp", _ld, w=stg_regions[sl])

        def _mm(e, sv=sv, blk=blk):
            for nn in range(2):
                n = blk * 2 + nn
                for kc in range(8):
                    ins = e.matmul(modps[:, n, :], lhsT=sv[:, kc, nn * 128:(nn + 1) * 128], rhs=scT[:, kc, :],
                                   start=(kc == 0), stop=(kc == 7))
            return ins
        S.add("pe", _mm, r=stg_regions[sl] + ["scT"], w=[("ps", TM)])
        emit_wtask()
        emit_wtask()
    while wi[0] < len(wtasks):
        emit_wtask()

    def _modev(e):
        for n in range(24):
            ins = e.activation(out=modT[:, n, :], in_=modps[:, n, :], func=AF.Identity, bias=bada[:, n:n + 1])
        return ins
    S.add("act", _modev, r=[("ps", TM), "bada"], w=["modT"])
    S.add("dve", lambda e: e.tensor_scalar(out=scale1T[:].rearrange("p a b -> p (a b)"),
                                           in0=modT[:, 8:16, :].rearrange("p a b -> p (a b)"), scalar1=1.0,
                                           scalar2=None, op0=ALU.add), r=["modT"], w=["scale1T"])
    W_IN_R = [("w_in", kc, p) for kc in range(8) for p in range(2)]
    W_OUT_R = [("w_out", kc) for kc in range(8)]
    stg_bf = stage[:, 0:4, :].rearrange("p a b -> p (a b)").bitcast(BF16)
    identb31 = identf[:].unsqueeze(1).to_broadcast([128, 31, 128])
    for c in range(4):
        sl = c % 2
        dv = stg_bf[:, sl * 4096:sl * 4096 + 31 * 128]
        eng = "dve" if c % 2 == 0 else "pool"
        S.add(eng, lambda e, dv=dv, c=c: e.tensor_tensor(
            out=dv.rearrange("p (a b) -> p a b", b=128), in0=identb31,
            in1=cwh[:, c * 31:(c + 1) * 31].unsqueeze(2).to_broadcast([128, 31, 128]), op=ALU.mult),
            r=["identf", ("tf", 3)], w=stg_regions[sl])
        S.add("sp", lambda e, dv=dv, c=c: e.dma_start(out=diag_d[:, c, 0:31 * 128], in_=dv), r=stg_regions[sl],
              w=[("diag_d", c)])

    lncnt = [0, 0]

    def layernorm_stats(src_ap, src_regions, width=1024, pool=0):
        sl = 2 * pool + lncnt[pool] % 2
        lncnt[pool] += 1
        nch = width // 512

        def _bn(e):
            for c in range(nch):
                ins = e.bn_stats(out=stats[:, sl, c, :], in_=src_ap[:, c * 512:(c + 1) * 512])
            return ins
        S.add("dve", _bn, r=src_regions, w=[("stats", sl)])
        S.add("dve", lambda e: e.bn_aggr(out=mv[:, sl, :], in_=stats[:, sl, 0:nch, :]), r=[("stats", sl)],
              w=[("mv", sl)])
        S.add("dve", lambda e: e.tensor_scalar(out=rs[:, sl, 0:1], in0=mv[:, sl, 1:2], scalar1=LN_EPS, scalar2=None,
                                               op0=ALU.add), r=[("mv", sl)], w=[("rs", sl)])
        S.add("pool", lambda e: e.tensor_tensor(out=rs[:, sl, 0:1], in0=rs[:, sl, 0:1], in1=expo[:, 0:1], op=ALU.pow),
              r=[("rs", sl), "expo"], w=[("rs", sl)])
        return sl

    xcnt = [0]

    def ln_part(src_dram_rows):
        xs = xcnt[0] % 2
        xcnt[0] += 1
        S.add("sp", lambda e: e.dma_start(out=xin(xs), in_=src_dram_rows), w=[("stg", xs)])
        sl = layernorm_stats(xin(xs), [("stg", xs)])
        S.add("pool", lambda e: e.tensor_scalar(out=rs[:, sl, 1:2], in0=mv[:, sl, 0:1], scalar1=-1.0,
                                                scalar2=rs[:, sl, 0:1], op0=ALU.mult, op1=ALU.mult),
              r=[("mv", sl), ("rs", sl)], w=[("rs2", sl)])
        S.add("pool", lambda e: e.tensor_scalar(out=xn_bf[:, xs, :], in0=xin(xs), scalar1=rs[:, sl, 0:1],
                                                scalar2=rs[:, sl, 1:2], op0=ALU.mult, op1=ALU.add),
              r=[("stg", xs), ("rs", sl), ("rs2", sl)], w=[("xn", xs)])
        return xs

    def tr_part(xs, bcol, dst_hT_ap, dst_regions):
        tb0 = next_p()
        tb1 = next_p()
        tp0 = ps[tb0][:].bitcast(BF16).rearrange("p (a b) -> p a b", b=128)
        tp1 = ps[tb1][:].bitcast(BF16).rearrange("p (a b) -> p a b", b=128)

        def _tr0(e):
            for kc in range(4):
                ins = e.transpose(out=tp0[:, kc, :], in_=xn_bf[:, xs, kc * 128:(kc + 1) * 128], identity=ident_bf)
            return ins
        S.add("pe", _tr0, r=[("xn", xs), "cb16"], w=[("ps", tb0)])

        def _tr1(e):
            for kc in range(4, 8):
                ins = e.transpose(out=tp1[:, kc - 4, :], in_=xn_bf[:, xs, kc * 128:(kc + 1) * 128], identity=ident_bf)
            return ins
        S.add("pe", _tr1, r=[("xn", xs), "cb16"], w=[("ps", tb1)])

        def _ev0(e):
            for kc in range(4):
                ins = e.activation(out=dst_hT_ap[:, kc, :], in_=tp0[:, kc, :], func=AF.Identity,
                                   scale=scale1T[:, kc, bcol:bcol + 1], bias=modT[:, kc, bcol:bcol + 1])
            return ins
        S.add("act", _ev0, r=[("ps", tb0), "scale1T", "modT"], w=[dst_regions[0]])

        def _ev1(e):
            for kc in range(4, 8):
                ins = e.tensor_scalar(out=dst_hT_ap[:, kc, :], in0=tp1[:, kc - 4, :], scalar1=scale1T[:, kc, bcol:bcol + 1],
                                      scalar2=modT[:, kc, bcol:bcol + 1], op0=ALU.mult, op1=ALU.add)
            return ins
        S.add("dve", _ev1, r=[("ps", tb1), "scale1T", "modT"], w=[dst_regions[1]])

    def HTR(hs):
        return [(("hT", hs), "lo"), (("hT", hs), "hi")]

    def proj_fm(col0, rhs_ap, rhs_regions, n):
        b = next_p()

        def _mm(e):
            for kc in range(8):
                ins = e.matmul(ps[b][:, 0:n], lhsT=w_in_bf[:, kc, col0:col0 + 128], rhs=rhs_ap[:, kc, :],
                               start=(kc == 0), stop=(kc == 7))
            return ins
        S.add("pe", _mm, r=W_IN_R + list(rhs_regions), w=[("ps", b)])
        return b

    def rope(b, csl, dst_ap, dst_regions):
        t1 = next_tf()
        t2 = next_tf()
        rb = next_p()
        S.add("act", lambda e: e.activation(out=ropeb[:], in_=ps[b][:], func=AF.Copy), r=[("ps", b)], w=["ropeb"])
        S.add("pe", lambda e: e.matmul(ps[rb][:], lhsT=R_bf, rhs=ropeb[:], start=True, stop=True),
              r=["ropeb", "cb16"], w=[("ps", rb)])
        S.add("dve", lambda e: e.tensor_tensor(out=TF[:, t1, :], in0=ps[b][:], in1=cs[:, csl, 0, :], op=ALU.mult),
              r=[("ps", b), ("cs", csl)], w=[("tf", t1)])
        S.add("dve", lambda e: e.tensor_tensor(out=TF[:, t2, :], in0=ps[rb][:], in1=cs[:, csl, 1, :], op=ALU.mult),
              r=[("ps", rb), ("cs", csl)], w=[("tf", t2)])
        S.add("pool", lambda e: e.tensor_tensor(out=dst_ap, in0=TF[:, t1, :], in1=TF[:, t2, :], op=ALU.add),
              r=[("tf", t1), ("tf", t2)], w=dst_regions)

    def gated_tanh(b, dst_ap, dst_regions):
        t = next_tf()
        S.add("act", lambda e: e.activation(out=TF[:, t, :], in_=ps[b][:], func=AF.Tanh, scale=0.5),
              r=[("ps", b)], w=[("tf", t)])
        S.add("dve", lambda e: e.scalar_tensor_tensor(out=dst_ap, in0=TF[:, t, :], scalar=1.0, in1=ps[b][:],
                                                      op0=ALU.add, op1=ALU.mult),
              r=[("tf", t), ("ps", b)], w=dst_regions)

    def stage_ctx(b, hs):
        hc = hT[:, hs, :, 0:CTXL]
        xs0 = ln_part(ctx_d[b * CTXL:b * CTXL + 128, :])
        xs1 = ln_part(ctx_d[b * CTXL + 128:b * CTXL + 256, :])
        tr_part(xs0, NB, hT[:, hs, :, 0:128], HTR(hs))
        tr_part(xs1, NB, hT[:, hs, :, 128:256], HTR(hs))
        pb = proj_fm(KO, hc, HTR(hs), CTXL)
        S.add("act", lambda e: e.activation(out=kT[:, SEQ:SEQ + CTXL], in_=ps[pb][:, 0:CTXL], func=AF.Copy),
              r=[("ps", pb)], w=[("k", NT), ("k", NT + 1)])
        vb = next_p()
        vps = ps[vb][:, 0:256].rearrange("p (a b) -> p a b", b=128)

        def _mm(e):
            for ct in range(2):
                for kc in range(8):
                    ins = e.matmul(vps[:, ct, :], lhsT=hT[:, hs, kc, ct * 128:(ct + 1) * 128],
                                   rhs=w_in_bf[:, kc, VO:VO + 128], start=(kc == 0), stop=(kc == 7))
            return ins
        S.add("pe", _mm, r=W_IN_R + HTR(hs), w=[("ps", vb)])

        def _ev(e):
            e.tensor_copy(out=Vaug[:, NT:NT + 2, 0:64], in_=vps[:, :, 0:64])
            return e.tensor_copy(out=Vaug[:, NT:NT + 2, 128:192], in_=vps[:, :, 64:128])
        S.add("dve", _ev, r=[("ps", vb)], w=[("V", NT), ("V", NT + 1)])

    def gate_broadcast(b):
        for half in range(2):
            gb = next_p()
            for c4 in range(4):
                kc = half * 4 + c4
                t = next_tf()
                S.add("dve", lambda e, kc=kc, t=t: e.tensor_copy(out=TF[:, t, 0:128],
                                                               in_=modT[:, 16 + kc, b:b + 1].to_broadcast([128, 128])),
                      r=["modT"], w=[("tf", t)])
                S.add("pe", lambda e, c4=c4, gb=gb, t=t: e.matmul(ps[gb][:, c4 * 128:(c4 + 1) * 128], lhsT=TF[:, t, 0:128],
                                                                 rhs=identf[:], start=True, stop=True),
                      r=[("tf", t), "identf"], w=[("ps", gb)])
            S.add("act", lambda e, gb=gb, half=half: e.activation(out=gate_bc[:, half * 512:(half + 1) * 512], in_=ps[gb][:],
                                                                func=AF.Copy), r=[("ps", gb)], w=["gate_bc"])

    prefetched = {}

    def stream_A(G):
        b, s = divmod(G, NS)
        hs = G % 2
        csl = G % 2
        S.add("sp", lambda e: e.dma_start(out=cs[:, csl, 0, :], in_=cos_d[:, s * 512:(s + 1) * 512]), w=[("cs", csl)])
        S.add("sp", lambda e: e.dma_start(out=cs[:, csl, 1, :], in_=sin_d[:, s * 512:(s + 1) * 512]), w=[("cs", csl)])
        r0 = b * SEQ + s * 512
        if G in prefetched:
            xsl = [prefetched.pop(G)]
        else:
            xsl = [ln_part(x_d[r0:r0 + 128, :])]
            yield
        for i in range(4):
            if i < 3:
                xsl.append(ln_part(x_d[r0 + (i + 1) * 128:r0 + (i + 2) * 128, :]))
            tr_part(xsl[i], b, hT[:, hs, :, i * 128:(i + 1) * 128], HTR(hs))
            yield
        hv = hT[:, hs, :, :]
        pb = proj_fm(KO, hv, HTR(hs), 512)
        rope(pb, csl, kT[:, s * 512:(s + 1) * 512], [("k", 4 * s + i) for i in range(4)])
        yield
        vb = next_p()
        vps = ps[vb][:].rearrange("p (a b) -> p a b", b=128)

        def _mm(e):
            for i in range(4):
                for kc in range(8):
                    ins = e.matmul(vps[:, i, :], lhsT=hT[:, hs, kc, i * 128:(i + 1) * 128],
                                   rhs=w_in_bf[:, kc, VO:VO + 128], start=(kc == 0), stop=(kc == 7))
            return ins
        S.add("pe", _mm, r=W_IN_R + HTR(hs), w=[("ps", vb)])

        def _ev(e):
            e.tensor_copy(out=Vaug[:, 4 * s:4 * s + 4, 0:64], in_=vps[:, :, 0:64])
            return e.tensor_copy(out=Vaug[:, 4 * s:4 * s + 4, 128:192], in_=vps[:, :, 64:128])
        S.add("dve", _ev, r=[("ps", vb)], w=[("V", 4 * s + i) for i in range(4)])
        yield
        for c in range(4):
            pcb = proj_fm(CBO + c * 128, hv, HTR(hs), 512)
            t = next_tf()
            S.add("act", lambda e, pcb=pcb, t=t: e.activation(out=TF[:, t, :], in_=ps[pcb][:], func=AF.Tanh, scale=0.5),
                  r=[("ps", pcb)], w=[("tf", t)])
            pca = proj_fm(CAO + c * 128, hv, HTR(hs), 512)
            S.add("dve", lambda e, pca=pca, t=t, c=c: e.scalar_tensor_tensor(
                out=uT[:, c, 15 + s * 512:15 + (s + 1) * 512], in0=TF[:, t, :], scalar=1.0, in1=ps[pca][:],
                op0=ALU.add, op1=ALU.mult), r=[("tf", t), ("ps", pca)], w=[("u", s)])
            yield
        if G + 1 < NG and not (G % NS == 0 and G > 0):
            b2, s2 = divmod(G + 1, NS)
            r2 = b2 * SEQ + s2 * 512
            prefetched[G + 1] = ln_part(x_d[r2:r2 + 128, :])
            yield

    ptc = [0]
    KHY = 1

    def stream_H(G):
        b, s = divmod(G, NS)
        hs = G % 2
        csl = G % 2
        ms = G % 2
        hv = hT[:, hs, :, :]
        for j in range(4):
            pb = proj_fm(QO + j * 128, hv, HTR(hs), 512)
            rope(pb, csl, qT[:, j, :], ["q"])
            yield
        for j in range(4):
            pb = proj_fm(AGO + j * 128, hv, HTR(hs), 512)
            gated_tanh(pb, sag2[:, j, :], [("sag2", j)])
            yield
        for i in range(4):
            t = 4 * s + i
            kts = [(t, None), (NT, None), (NT + 1, None)]
            if t > 0:
                kts.append((t - 1, maskA))
            if t < NT - 1:
                kts.append((t + 1, maskB))
                if i == 3:
                    yield "needA"
            if True:
                pairs = [(g, kt, m) for (kt, m) in kts for g in range(2)]
            else:
                pairs = [(g, kt, m) for g in range(2) for (kt, m) in kts]
            nk = len(kts)
            info = []

            def emit_S(n, i=i):
                g, kt, m = pairs[n]
                sbk = SB_[n % 2]
                def _smm(e, g=g, kt=kt, sbk=sbk, m=m, i=i):
                    o = ps[sbk][:].rearrange("p (a b) -> p a b", b=128)
                    ins = e.matmul(o, lhsT=kT[g * 64:(g + 1) * 64, kt * 128:(kt + 1) * 128],
                                   rhs=qT[g * 64:(g + 1) * 64, :, i * 128:(i + 1) * 128], start=True, stop=(m is None or not MASK_PE))
                    if m is not None and MASK_PE:
                        ins = e.matmul(o, lhsT=ident_bf, rhs=m.unsqueeze(1).to_broadcast([128, 4, 128]), start=False,
                                       stop=True)
                    return ins
                S.add("pe", _smm, r=[("k", kt), "q", "cb16"], w=[("ps", sbk)])
                p = ptc[0] % NPT
                ptc[0] += 1
                S.add("act", lambda e, sbk=sbk, p=p: e.activation(out=PT[:, p, :], in_=ps[sbk][:], func=AF.Exp,
                                                                  scale=0.125), r=[("ps", sbk)], w=[("pt", p)])
                if m is not None and not MASK_PE:
                    S.add("dve", lambda e, p=p, m=m: e.tensor_tensor(
                        out=PT[:, p, :].rearrange("p (a b) -> p a b", b=128),
                        in0=PT[:, p, :].rearrange("p (a b) -> p a b", b=128),
                        in1=m.unsqueeze(1).to_broadcast([128, 4, 128]), op=ALU.mult),
                        r=[("pt", p), "cb16"], w=[("pt", p)])
                info.append(p)

            def emit_PV(n):
                g, kt, m = pairs[n]
                p = info[n]
                first = (n < 2) if True else (n % nk == 0)
                S.add("pe", lambda e, g=g, kt=kt, p=p, first=first: e.matmul(
                    ps[OB[g]][:], lhsT=Vaug[:, kt, 64 * g:64 * g + 128], rhs=PT[:, p, :], start=first, stop=False),
                    r=[("V", kt), ("pt", p)], w=[("ps", OB[g])])
                if (n >= len(pairs) - 2) if True else (n % nk == nk - 1):
                    S.add("pe", lambda e, g=g: e.matmul(
                        ps[OB[g]][:].rearrange("p (a b) -> p a b", b=128), lhsT=sinkL[:, g, :],
                        rhs=esrow[:, 4 * g:4 * g + 4].unsqueeze(2).to_broadcast([1, 4, 128]), start=False, stop=True),
                        r=["sinkL", "esrow"], w=[("ps", OB[g])])

            emit_S(0)
            emit_S(1)
            for n in range(0, len(pairs), 2):
                if n + 2 < len(pairs):
                    emit_S(n + 2)
                    emit_S(n + 3)
                emit_PV(n)
                emit_PV(n + 1)
                if (n // 2) % KHY == KHY - 1:
                    yield
            S.add("act", lambda e: e.activation(out=denA[0:64, :], in_=ps[OB[0]][64:128, :], func=AF.Copy),
                  r=[("ps", OB[0])], w=["denA0"])
            S.add("act", lambda e: e.activation(out=denA[64:128, :], in_=ps[OB[1]][0:64, :], func=AF.Copy),
                  r=[("ps", OB[1])], w=["denA1"])
            S.add("act", lambda e: e.activation(out=numA[0:64, :], in_=ps[OB[0]][0:64, :], func=AF.Copy),
                  r=[("ps", OB[0])], w=["numA0"])
            S.add("act", lambda e: e.activation(out=numA[64:128, :], in_=ps[OB[1]][64:128, :], func=AF.Copy),
                  r=[("ps", OB[1])], w=["numA1"])
            S.add("dve", lambda e: e.reciprocal(out=denA[:], in_=denA[:]), r=["denA0", "denA1"], w=["denA0", "denA1"])
            ta = next_tf()
            S.add("dve", lambda e, ta=ta: e.scalar_tensor_tensor(out=TF[:, ta, :], in0=numA[:], scalar=0.5, in1=denA[:],
                                                                 op0=ALU.mult, op1=ALU.mult),
                  r=["numA0", "numA1", "denA0", "denA1"], w=[("tf", ta)])
            S.add("pool", lambda e, i=i, ta=ta: e.tensor_tensor(
                out=mixA[:, ms, :, i * 128:(i + 1) * 128], in0=TF[:, ta, :].rearrange("p (a b) -> p a b", b=128),
                in1=sag2[:, :, i * 128:(i + 1) * 128], op=ALU.mult),
                r=[("tf", ta)] + [("sag2", j) for j in range(4)], w=[("mixA", ms, i)])
            yield
        yield "needT"
        for c in range(4):
            pb = proj_fm(CGO + c * 128, hv, HTR(hs), 512)
            gated_tanh(pb, scg2[:, c, :], [("scg2", c)])
            yield

    dgc = [0]
    fcnt = [0]

    def stream_T(G):
        b, s = divmod(G, NS)
        ms = G % 2
        if s == 0:
            gate_broadcast(b)
            yield
        for c in range(4):
            cbk = next_p()
            for j0 in range(0, 31, 4):
                nj = min(4, 31 - j0)
                dg = dgc[0] % NDG
                dgc[0] += 1
                S.add("sp", lambda e, dg=dg, c=c, j0=j0, nj=nj: e.dma_start(
                    out=diag[:, dg, 0:nj, :].rearrange("p a b -> p (a b)"),
                    in_=diag_d[:, c, j0 * 128:(j0 + nj) * 128]), r=[("diag_d", c)], w=[("diag", dg)])

                def _cm(e, dg=dg, c=c, j0=j0, nj=nj, cbk=cbk):
                    for jj in range(nj):
                        j = j0 + jj
                        ins = e.matmul(ps[cbk][:], lhsT=diag[:, dg, jj, :], rhs=uT[:, c, s * 512 + j:s * 512 + j + 512],
                                       start=(j == 0), stop=(j == 30))
                    return ins
                S.add("pe", _cm, r=[("diag", dg), ("u", s - 1), ("u", s), ("u", s + 1)], w=[("ps", cbk)])
            S.add("act", lambda e, c=c, cbk=cbk: e.activation(out=convf[:, c, :], in_=ps[cbk][:], func=AF.Identity,
                                                              bias=cvec[:, c:c + 1]), r=[("ps", cbk), "cvec"],
                  w=[("convf", c)])
            S.add("act", lambda e, c=c, cbk=cbk: e.activation(out=mixC[:, c, :], in_=ps[cbk][:], func=AF.Square,
                                                              bias=cvec[:, c:c + 1]), r=[("ps", cbk), "cvec"],
                  w=[("mixC", c)])
            S.add("pool", lambda e, c=c: e.tensor_copy(out=conv_bf[:, c, :], in_=convf[:, c, :]), r=[("convf", c)],
                  w=[("cbf", c)])
            yield
        sbk = next_p()

        def _st(e):
            for a in range(4):
                for c in range(4):
                    e.matmul(ps[sbk][:, a:a + 1], lhsT=conv_bf[:, c, a * 128:(a + 1) * 128], rhs=ones512[:, 0:1],
                             start=(c == 0), stop=(c == 3))
            for a in range(4):
                for c in range(4):
                    ins = e.matmul(ps[sbk][:, 4 + a:5 + a], lhsT=mixC[:, c, a * 128:(a + 1) * 128],
                                   rhs=ones512[:, 0:1], start=(c == 0), stop=(c == 3))
            return ins
        S.add("pe", _st, r=["ones512"] + [("cbf", c) for c in range(4)] + [("mixC", c) for c in range(4)],
              w=[("ps", sbk)])
        S.add("dve", lambda e: e.tensor_copy(out=st8[:, 0:8], in_=ps[sbk][:, 0:8]), r=[("ps", sbk)], w=["st8a"])
        S.add("dve", lambda e: e.tensor_tensor(out=st8[:, 8:12], in0=st8[:, 0:4], in1=st8[:, 0:4], op=ALU.mult),
              r=["st8a"], w=["st8b"])
        S.add("dve", lambda e: e.scalar_tensor_tensor(out=st8[:, 12:16], in0=st8[:, 4:8], scalar=LN_EPS, in1=st8[:, 8:12],
                                                      op0=ALU.add, op1=ALU.subtract), r=["st8a", "st8b"], w=["st8c"])
        S.add("pool", lambda e: e.tensor_tensor(out=st8[:, 12:16], in0=st8[:, 12:16], in1=expo[:, 0:4], op=ALU.pow),
              r=["st8c", "expo"], w=["st8c"])
        S.add("dve", lambda e: e.scalar_tensor_tensor(out=st8[:, 16:20], in0=st8[:, 0:4], scalar=-1.0, in1=st8[:, 12:16],
                                                      op0=ALU.mult, op1=ALU.mult), r=["st8a", "st8c"], w=["st8d"])
        tm = next_tf()
        tv = next_tf()
        identb = identf[:].unsqueeze(1).to_broadcast([128, 4, 128])
        S.add("dve", lambda e: e.tensor_tensor(out=TF[:, tm, :].rearrange("p (a b) -> p a b", b=128), in0=identb,
                                               in1=st8[:, 12:16].unsqueeze(2).to_broadcast([128, 4, 128]), op=ALU.mult),
              r=["identf", "st8c"], w=[("tf", tm)])
        S.add("dve", lambda e: e.tensor_tensor(out=TF[:, tv, :].rearrange("p (a b) -> p a b", b=128), in0=identb,
                                               in1=st8[:, 16:20].unsqueeze(2).to_broadcast([128, 4, 128]), op=ALU.mult),
              r=["identf", "st8d"], w=[("tf", tv)])
        rbk = next_p()
        nbk = next_p()
        S.add("pe", lambda e: e.matmul(ps[rbk][:], lhsT=onesf[:], rhs=TF[:, tm, :], start=True, stop=True),
              r=["onesf", ("tf", tm)], w=[("ps", rbk)])
        S.add("pe", lambda e: e.matmul(ps[nbk][:], lhsT=onesf[:], rhs=TF[:, tv, :], start=True, stop=True),
              r=["onesf", ("tf", tv)], w=[("ps", nbk)])
        def emit_z(c):
            S.add("dve", lambda e: e.tensor_scalar(out=convf[:, c, :], in0=convf[:, c, :], scalar1=cvec[:, 4 + c:5 + c],
                                                   scalar2=cvec[:, 8 + c:9 + c], op0=ALU.mult, op1=ALU.add),
                  r=[("convf", c), "cvec"], w=[("convf", c)])

        for c in range(4):
            S.add("dve", lambda e, c=c: e.tensor_tensor(out=convf[:, c, :], in0=convf[:, c, :], in1=ps[rbk][:],
                                                        op=ALU.mult), r=[("convf", c), ("ps", rbk)],
                  w=[("convf", c)])
            S.add("dve", lambda e, c=c: e.tensor_tensor(out=convf[:, c, :], in0=convf[:, c, :], in1=ps[nbk][:],
                                                        op=ALU.add), r=[("convf", c), ("ps", nbk)], w=[("convf", c)])
        emit_z(0)
        emit_z(1)
        yield
        for c in range(4):
            tz = next_tf()
            S.add("act", lambda e, c=c, tz=tz: e.activation(out=TF[:, tz, :], in_=convf[:, c, :], func=AF.Tanh, scale=0.5),
                  r=[("convf", c)], w=[("tf", tz)])
            if c + 2 < 4:
                emit_z(c + 2)
            S.add("dve", lambda e, c=c, tz=tz: e.scalar_tensor_tensor(out=TF[:, tz, :], in0=TF[:, tz, :], scalar=1.0,
                                                                      in1=convf[:, c, :], op0=ALU.add, op1=ALU.mult),
                  r=[("tf", tz), ("convf", c)], w=[("tf", tz)])
            S.add("dve", lambda e, c=c, tz=tz: e.scalar_tensor_tensor(out=mixC[:, c, :], in0=TF[:, tz, :], scalar=0.25,
                                                                      in1=scg2[:, c, :], op0=ALU.mult, op1=ALU.mult),
                  r=[("tf", tz), ("scg2", c)], w=[("mixC", c)])
            yield
        for i in range(4):
            fs = fcnt[0] % 2
            fcnt[0] += 1
            r0 = b * SEQ + s * 512 + i * 128
            S.add("sp", lambda e, r0=r0: e.dma_start(out=xres, in_=x_d[r0:r0 + 128, :]), w=[("stg", 2)])
            for half in range(2):
                yb = next_p()

                def _mm(e, yb=yb, half=half, i=i):
                    for kc in range(4):
                        e.matmul(ps[yb][:], lhsT=mixA[:, ms, kc, i * 128:(i + 1) * 128],
                                 rhs=w_out_bf[:, kc, half * 512:(half + 1) * 512], start=(kc == 0), stop=False)
                    for kc in range(4):
                        ins = e.matmul(ps[yb][:], lhsT=mixC[:, kc, i * 128:(i + 1) * 128],
                                       rhs=w_out_bf[:, 4 + kc, half * 512:(half + 1) * 512], start=False, stop=(kc == 3))
                    return ins
                S.add("pe", _mm, r=W_OUT_R + [("mixA", ms, i)] + [("mixC", c) for c in range(4)], w=[("ps", yb)])
                S.add("dve", lambda e, yb=yb, half=half, fs=fs: e.tensor_tensor(
                    out=Rt(fs)[:, half * 512:(half + 1) * 512], in0=ps[yb][:], in1=gate_bc[:, half * 512:(half + 1) * 512],
                    op=ALU.mult), r=[("ps", yb), "gate_bc"], w=[("stg", 3 + fs)])
            yield
            S.add("dve", lambda e, fs=fs: e.scalar_tensor_tensor(out=Rt(fs), in0=xres, scalar=ALPHA, in1=Rt(fs),
                                                                 op0=ALU.mult, op1=ALU.add),
                  r=[("stg", 2), ("stg", 3 + fs)], w=[("stg", 3 + fs)])
            sl = layernorm_stats(Rt(fs), [("stg", 3 + fs)], pool=1)
            yield
            S.add("dve", lambda e, sl=sl, fs=fs: e.tensor_scalar(out=Rt(fs), in0=Rt(fs), scalar1=mv[:, sl, 0:1],
                                                                 scalar2=rs[:, sl, 0:1], op0=ALU.subtract, op1=ALU.mult),
                  r=[("stg", 3 + fs), ("mv", sl), ("rs", sl)], w=[("stg", 3 + fs)])
            yield
            S.add("pool", lambda e, fs=fs: e.tensor_tensor(out=Rt(fs), in0=Rt(fs), in1=g_bc[:], op=ALU.mult),
                  r=[("stg", 3 + fs), "g_bc"], w=[("stg", 3 + fs)])
            yield
            S.add("pool", lambda e, fs=fs: e.tensor_tensor(out=Rt(fs), in0=Rt(fs), in1=b_bc[:], op=ALU.add),
                  r=[("stg", 3 + fs), "b_bc"], w=[("stg", 3 + fs)])
            S.add("pool", lambda e, fs=fs, r0=r0: e.dma_start(out=out_d[r0:r0 + 128, :], in_=Rt(fs)),
                  r=[("stg", 3 + fs)], w=[("out", r0)], q="poolq")
            yield

    def drain(g):
        if g is None:
            return
        for _ in g:
            pass

    stage_ctx(0, 1)
    for G in range(NG + 2):
        gA = stream_A(G) if G < NG else None
        gH = stream_H(G - 1) if 1 <= G <= NG else None
        gT = stream_T(G - 2) if 2 <= G <= NG + 1 else None
        live = {"A": gA, "H": gH, "T": gT}
        order = ("H", "A", "T")
        while any(v is not None for v in live.values()):
            for name in order:
                g = live[name]
                if g is None:
                    continue
                try:
                    v = next(g)
                except StopIteration:
                    live[name] = None
                    continue
                if v == "needA":
                    drain(live["A"])
                    live["A"] = None
                elif v == "needT":
                    drain(live["T"])
                    live["T"] = None
        if G < NG and G > 0 and G % NS == 0:
            stage_ctx(G // NS, (G + 1) % 2)

    out_regions = [("out", b * SEQ + t * 128) for b in range(NB) for t in range(NT)]
    S.add("sp", None, r=out_regions)

    S.analyse()
    sems = {}
    for k in S.sem_keys():
        nm = "s_" + (k if isinstance(k, str) else f"{k[0]}{k[1]}")
        sems[k] = st.enter_context(nc.semaphore(nm))
    with nc.Block() as block:
        @block.tensor
        def _(e):
            S.emit_engine("pe", e, sems)

        @block.scalar
        def _(e):
            S.emit_engine("act", e, sems)

        @block.vector
        def _(e):
            S.emit_engine("dve", e, sems)

        @block.gpsimd
        def _(e):
            S.emit_engine("pool", e, sems)

        @block.sync
        def _(e):
            S.emit_engine("sp", e, sems)
    st.close()
    return nc


def _head_perm():
    idx = []
    for j in range(4):
        idx += list(range(j * 64, (j + 1) * 64)) + list(range((4 + j) * 64, (5 + j) * 64))
    return np.array(idx)


def _rope_tables(seq):
    grid_w = 64
    rows = seq // grid_w
    row = np.repeat(np.arange(rows, dtype=np.float32), grid_w)
    col = np.tile(np.arange(grid_w, dtype=np.float32), rows)
    inv_freq = (np.float32(10000.0) ** (-np.arange(0, 32, 2, dtype=np.float32) / np.float32(32))).astype(np.float32)
    cosT = np.zeros((128, seq), np.float32)
    sinT = np.zeros((128, seq), np.float32)
    for p in range(128):
        d = p % 64
        pos = row if d < 32 else col
        i = d % 32
        ang = (pos * inv_freq[i % 16]).astype(np.float32)
        cosT[p] = np.cos(ang)
        sinT[p] = np.sin(ang) * (-1.0 if i < 16 else 1.0)
    return cosT, sinT


def _const_mats():
    ident = np.eye(128, dtype=np.float32)
    R = np.zeros((128, 128), np.float32)
    for m in range(128):
        partner = m + 16 if (m % 32) < 16 else m - 16
        R[partner, m] = 1.0
    k = np.arange(128)[:, None]
    q = np.arange(128)[None, :]
    maskA = (k >= q).astype(np.float32)
    maskB = (k <= q).astype(np.float32)
    return np.concatenate([ident, R, maskA, maskB], axis=1)


def make_in_maps(inputs, ncores, NB, SEQ):
    f = lambda a: np.ascontiguousarray(np.asarray(a, dtype=np.float32))
    x = f(inputs["x"])
    c = f(inputs["c"])
    ctx = f(inputs["ctx"])
    c_ctx = f(inputs["c_ctx"])
    w_ada = f(inputs["w_ada"])[0]
    b_ada = f(inputs["b_ada"])[0]
    w_in = f(inputs["w_in"])[0]
    w_out = f(inputs["w_out"])[0]
    sink = f(inputs["attn_sink"])[0]
    conv_w = f(inputs["conv_w"])[0]
    hp = _head_perm()
    cols = np.concatenate([hp, 512 + np.arange(256), 768 + hp, 1280 + np.arange(1536)])
    w_in_p = f(w_in[:, cols])
    rows = np.concatenate([hp, 512 + np.arange(512)])
    w_out_p = f(w_out[rows, :])
    b_adaT = f(b_ada.reshape(24, 128).T)
    sinkp = f(np.concatenate([sink[0:4], sink[4:8]])[None, :])
    conv_wT = f(conv_w.T.reshape(4, 128, 31).transpose(1, 0, 2).reshape(128, 124))
    vec = lambda v: f(inputs[v])[0].reshape(4, 128).T
    cvec = f(np.concatenate([vec("conv_b"), vec("conv_ln_g"), vec("conv_ln_b")], axis=1))
    post_g = f(inputs["post_ln_g"])[0][None, :]
    post_b = f(inputs["post_ln_b"])[0][None, :]
    cosT, sinT = _rope_tables(SEQ)
    cmat = _const_mats()
    maps = []
    for ci in range(ncores):
        b0 = ci * NB
        cc = np.concatenate([c[b0:b0 + NB], c_ctx[None, :]], axis=0)
        cT = f(cc.T.reshape(8, 128, NB + 1).transpose(1, 0, 2).reshape(128, 8 * (NB + 1)))
        maps.append({
            "x": f(x[b0:b0 + NB].reshape(NB * SEQ, D)),
            "ctx": f(ctx[b0:b0 + NB].reshape(NB * CTXL, D)),
            "cT": cT, "w_ada": w_ada, "b_adaT": b_adaT, "w_in": w_in_p, "w_out": w_out_p, "sinkp": sinkp,
            "conv_wT": conv_wT, "cvec": cvec, "post_g": post_g, "post_b": post_b, "cosT": cosT, "sinT": sinT,
            "cmat": cmat,
        })
    return maps


def kernel(**inputs):
    B, SEQ, _ = inputs["x"].shape
    NB = B // NCORES
    nc = build(NB, SEQ)
    maps = make_in_maps(inputs, NCORES, NB, SEQ)
    res = run_bass_kernel_spmd(nc, maps, core_ids=list(range(NCORES)))
    outs = [r["out"].reshape(NB, SEQ, D) for r in res.results]
    return np.concatenate(outs, axis=0).astype(np.float32)
```
